# Optimizing a Trainium2 kernel written in Bass

```python
import math
import jax
import jax.numpy as jnp
from jax import lax
import numpy as np

D_MODEL = 1024
BATCH = 8
SEQ = 4096
DEPTH = 2

CTX_LEN = 256
GRID_W = 64
HEAD_DIM = 64
ROPE_BASE = 10000.0
Q_BLOCK = 128
EPS = 1e-6
N_MOD = 9
N_BRANCH = 4
BRANCH_W = D_MODEL // 2
A_Q_HEADS = BRANCH_W // HEAD_DIM
A_KV_HEADS = 2
A_GROUP = A_Q_HEADS // A_KV_HEADS
B_CH = BRANCH_W
CONV_W = 31
C_HEADS = BRANCH_W // (2 * HEAD_DIM)
C_V_DIM = 2 * HEAD_DIM
D_GROUPS = 4
D_GROUP_CH = BRANCH_W // D_GROUPS
D_FF = 256 * ((8 * D_MODEL // 3 + 255) // 256)
ATTN_SCALE = HEAD_DIM ** -0.5
IN_SIZES = (A_Q_HEADS * HEAD_DIM, A_KV_HEADS * HEAD_DIM, A_KV_HEADS * HEAD_DIM,
            2 * B_CH,
            C_HEADS * 2 * HEAD_DIM, C_HEADS * 2 * HEAD_DIM, C_HEADS * C_V_DIM,
            D_GROUPS * D_GROUP_CH,
            N_BRANCH * D_MODEL)
IN_COLS = sum(IN_SIZES)

kernel_name = 'hybrid_prefix_dit_block'


def rms_norm(x, g):
    xf = x.astype(jnp.float32)
    y = xf * lax.rsqrt(jnp.mean(xf * xf, axis=-1, keepdims=True) + EPS)
    return (y * g.astype(jnp.float32)).astype(x.dtype)


def layer_norm(x, g, b):
    xf = x.astype(jnp.float32)
    mu = jnp.mean(xf, axis=-1, keepdims=True)
    xc = xf - mu
    var = jnp.mean(xc * xc, axis=-1, keepdims=True)
    return (xc * lax.rsqrt(var + EPS) * g.astype(jnp.float32) + b.astype(jnp.float32)).astype(x.dtype)


def modulate(x, shift, scale):
    return x * (1 + scale) + shift


def adaln(cond, w, b):
    m = jax.nn.silu(cond) @ w + b
    return m.reshape(cond.shape[0], N_MOD, 1, D_MODEL)


def swiglu(x, w1, w3, w2):
    return (jax.nn.silu(x @ w1) * (x @ w3)) @ w2


def half_ffn(h, shift, scale, gate, g, w1, w3, w2):
    n = modulate(rms_norm(h, g), shift, scale)
    return h + 0.5 * gate * swiglu(n, w1, w3, w2)


def split_cols(z):
    offs, acc = [], 0
    for s in IN_SIZES[:-1]:
        acc += s
        offs.append(acc)
    return jnp.split(z, offs, axis=-1)


def axial_rope_tables(n_tokens):
    rows = n_tokens // GRID_W
    row = jnp.broadcast_to(jnp.arange(rows)[:, None], (rows, GRID_W)).reshape(-1).astype(jnp.float32)
    col = jnp.broadcast_to(jnp.arange(GRID_W)[None, :], (rows, GRID_W)).reshape(-1).astype(jnp.float32)
    half = HEAD_DIM // 2
    inv = ROPE_BASE ** (-jnp.arange(0, half, 2, dtype=jnp.float32) / half)
    ang_r = row[:, None] * inv
    ang_c = col[:, None] * inv
    ang = jnp.concatenate([ang_r, ang_r, ang_c, ang_c], axis=-1)
    return jnp.cos(ang), jnp.sin(ang)


def apply_rope(x, cos, sin):
    shape = (x.shape[1],) + (1,) * (x.ndim - 3) + (HEAD_DIM,)
    cos, sin = cos.reshape(shape), sin.reshape(shape)
    x1, x2, x3, x4 = jnp.split(x, 4, axis=-1)
    rot = jnp.concatenate([-x2, x1, -x4, x3], axis=-1)
    return (x * cos + rot * sin).astype(x.dtype)


def blockwise(fn, q, k, v):
    b, s = q.shape[:2]
    nb = s // Q_BLOCK
    qb = jnp.moveaxis(q.reshape((b, nb, Q_BLOCK) + q.shape[2:]), 1, 0)
    out = lax.map(lambda qi: fn(qi, k, v), qb)
    out = jnp.moveaxis(out, 0, 1)
    return out.reshape((b, s) + out.shape[3:])


def gqa_heads(aq, ak, av, qk_g):
    b, t = aq.shape[:2]
    q = rms_norm(aq.reshape(b, t, A_KV_HEADS, A_GROUP, HEAD_DIM), qk_g[0])
    k = rms_norm(ak.reshape(b, t, A_KV_HEADS, HEAD_DIM), qk_g[1])
    v = av.reshape(b, t, A_KV_HEADS, HEAD_DIM)
    return q, k, v


def gqa_attend(q, k, v):
    s = jnp.einsum('bqkgd,btkd->bkgqt', q, k).astype(jnp.float32) * ATTN_SCALE
    p = jax.nn.softmax(s, axis=-1).astype(v.dtype)
    return jnp.einsum('bkgqt,btkd->bqkgd', p, v)


def diff_heads(cq, ck, cv):
    b, t = cq.shape[:2]
    q = cq.reshape(b, t, C_HEADS, 2, HEAD_DIM)
    k = ck.reshape(b, t, C_HEADS, 2, HEAD_DIM)
    v = cv.reshape(b, t, C_HEADS, C_V_DIM)
    return q, k, v


def diff_attend(q, k, v, lam):
    s = jnp.einsum('bqhpd,bthpd->bhpqt', q, k).astype(jnp.float32) * ATTN_SCALE
    p = jax.nn.softmax(s, axis=-1)
    w = (p[:, :, 0] - lam * p[:, :, 1]).astype(v.dtype)
    return jnp.einsum('bhqt,bthe->bqhe', w, v)


def diff_out(o, g, lam_init):
    o = rms_norm(o, g) * (1.0 - lam_init)
    return o.reshape(o.shape[:2] + (BRANCH_W,))


def conv_module(u, conv_w, conv_b, ln_g, ln_b):
    a, g = jnp.split(u, 2, axis=-1)
    y = a * jax.nn.sigmoid(g)
    y = lax.conv_general_dilated(y, conv_w[:, None, :], window_strides=(1,),
                                 padding=[(CONV_W // 2, CONV_W // 2)],
                                 dimension_numbers=('NWC', 'WIO', 'NWC'),
                                 feature_group_count=B_CH) + conv_b
    return jax.nn.silu(layer_norm(y, ln_g, ln_b))


def fourier_mix(u):
    b, t = u.shape[:2]
    z = u.astype(jnp.float32).reshape(b, t, D_GROUPS, D_GROUP_CH)
    f = jnp.fft.fft2(z, axes=(1, 3), norm='ortho').real
    return f.reshape(b, t, BRANCH_W).astype(u.dtype)


def merge(y_a, y_b, y_c, y_d, gate_logits, w_branch, w_out):
    y = jnp.stack([y_a, y_b, y_c, y_d], axis=2)
    proj = jnp.einsum('btnc,ncd->btnd', y, w_branch)
    g = jax.nn.sigmoid(gate_logits.reshape(gate_logits.shape[:2] + (N_BRANCH, D_MODEL)))
    return jnp.einsum('btnd,btnd->btd', g, proj) @ w_out


def token_mixing(n, nc, w_in, qk_norm_a, conv_w, conv_b, conv_ln_g, conv_ln_b,
                 diff_lam, diff_subln_g, w_branch, w_out, lam_init, cos, sin, update_ctx):
    aq, ak, av, bu, cq, ck, cv, du, gl = split_cols(n @ w_in)
    aqc, akc, avc, buc, cqc, ckc, cvc, duc, glc = split_cols(nc @ w_in)
    lp = diff_lam.astype(jnp.float32)
    lam = jnp.exp(jnp.sum(lp[0] * lp[1])) - jnp.exp(jnp.sum(lp[2] * lp[3])) + lam_init

    qa, ka, va = gqa_heads(aq, ak, av, qk_norm_a)
    qa, ka = apply_rope(qa, cos, sin), apply_rope(ka, cos, sin)
    qac, kac, vac = gqa_heads(aqc, akc, avc, qk_norm_a)
    y_a = blockwise(gqa_attend, qa, jnp.concatenate([ka, kac], axis=1), jnp.concatenate([va, vac], axis=1))
    y_a = y_a.reshape(y_a.shape[:2] + (BRANCH_W,))

    qd, kd, vd = diff_heads(cq, ck, cv)
    qd, kd = apply_rope(qd, cos, sin), apply_rope(kd, cos, sin)
    qdc, kdc, vdc = diff_heads(cqc, ckc, cvc)
    y_c = blockwise(lambda qi, k, v: diff_attend(qi, k, v, lam), qd,
                    jnp.concatenate([kd, kdc], axis=1), jnp.concatenate([vd, vdc], axis=1))
    y_c = diff_out(y_c, diff_subln_g, lam_init)

    y_b = conv_module(bu, conv_w, conv_b, conv_ln_g, conv_ln_b)
    y_d = fourier_mix(du)
    y = merge(y_a, y_b, y_c, y_d, gl, w_branch, w_out)
    if not update_ctx:
        return y, None

    y_a_ctx = gqa_attend(qac, kac, vac)
    y_a_ctx = y_a_ctx.reshape(y_a_ctx.shape[:2] + (BRANCH_W,))
    y_c_ctx = diff_out(diff_attend(qdc, kdc, vdc, lam), diff_subln_g, lam_init)
    y_b_ctx = conv_module(buc, conv_w, conv_b, conv_ln_g, conv_ln_b)
    y_d_ctx = fourier_mix(duc)
    y_ctx = merge(y_a_ctx, y_b_ctx, y_c_ctx, y_d_ctx, glc, w_branch, w_out)
    return y, y_ctx


def setup_inputs(seed: int = 0) -> dict:
    key = jax.random.key(seed)
    ks = jax.random.split(key, 21)

    def nrm(k, shape, scale):
        return jax.random.normal(k, shape, jnp.float32) * scale

    return {
        'x': nrm(ks[0], (BATCH, SEQ, D_MODEL), 1.0),
        'c': nrm(ks[1], (BATCH, D_MODEL), 1.0),
        'ctx': nrm(ks[2], (BATCH, CTX_LEN, D_MODEL), 1.0),
        'c_ctx': nrm(ks[3], (D_MODEL,), 1.0),
        'ada_w': nrm(ks[4], (DEPTH, D_MODEL, N_MOD * D_MODEL), 0.5 * D_MODEL ** -0.5),
        'ada_b': nrm(ks[5], (DEPTH, N_MOD * D_MODEL), 0.01),
        'norm_g': 1.0 + nrm(ks[6], (DEPTH, 3, D_MODEL), 0.02),
        'ffn_w1': nrm(ks[7], (DEPTH, 2, D_MODEL, D_FF), D_MODEL ** -0.5),
        'ffn_w3': nrm(ks[8], (DEPTH, 2, D_MODEL, D_FF), D_MODEL ** -0.5),
        'ffn_w2': nrm(ks[9], (DEPTH, 2, D_FF, D_MODEL), D_FF ** -0.5),
        'w_in': nrm(ks[10], (DEPTH, D_MODEL, IN_COLS), D_MODEL ** -0.5),
        'qk_norm_a': 1.0 + nrm(ks[11], (DEPTH, 2, HEAD_DIM), 0.02),
        'conv_w': nrm(ks[12], (DEPTH, CONV_W, B_CH), CONV_W ** -0.5),
        'conv_b': nrm(ks[13], (DEPTH, B_CH), 0.01),
        'conv_ln_g': 1.0 + nrm(ks[14], (DEPTH, B_CH), 0.02),
        'conv_ln_b': nrm(ks[15], (DEPTH, B_CH), 0.01),
        'diff_lam': nrm(ks[16], (DEPTH, 4, HEAD_DIM), 0.1),
        'diff_subln_g': 1.0 + nrm(ks[17], (DEPTH, C_V_DIM), 0.02),
        'w_branch': nrm(ks[18], (DEPTH, N_BRANCH, BRANCH_W, D_MODEL), BRANCH_W ** -0.5),
        'w_out': nrm(ks[19], (DEPTH, D_MODEL, D_MODEL), D_MODEL ** -0.5),
        'final_g': 1.0 + nrm(ks[20], (D_MODEL,), 0.02),
    }


def reference(x, c, ctx, c_ctx, ada_w, ada_b, norm_g, ffn_w1, ffn_w3, ffn_w2, w_in, qk_norm_a,
              conv_w, conv_b, conv_ln_g, conv_ln_b, diff_lam, diff_subln_g, w_branch, w_out, final_g):
    cos, sin = axial_rope_tables(x.shape[1])
    h, hc = x, ctx
    for l in range(DEPTH):
        update_ctx = l < DEPTH - 1
        lam_init = 0.8 - 0.6 * math.exp(-0.3 * l)
        m = adaln(c, ada_w[l], ada_b[l])
        mc = adaln(c_ctx[None, :], ada_w[l], ada_b[l])
        h = half_ffn(h, m[:, 0], m[:, 1], m[:, 2], norm_g[l, 0], ffn_w1[l, 0], ffn_w3[l, 0], ffn_w2[l, 0])
        hc = half_ffn(hc, mc[:, 0], mc[:, 1], mc[:, 2], norm_g[l, 0], ffn_w1[l, 0], ffn_w3[l, 0], ffn_w2[l, 0])
        n = modulate(rms_norm(h, norm_g[l, 1]), m[:, 3], m[:, 4])
        nc = modulate(rms_norm(hc, norm_g[l, 1]), mc[:, 3], mc[:, 4])
        y, y_ctx = token_mixing(n, nc, w_in[l], qk_norm_a[l], conv_w[l], conv_b[l], conv_ln_g[l],
                                conv_ln_b[l], diff_lam[l], diff_subln_g[l], w_branch[l], w_out[l],
                                lam_init, cos, sin, update_ctx)
        h = h + m[:, 5] * y
        h = half_ffn(h, m[:, 6], m[:, 7], m[:, 8], norm_g[l, 2], ffn_w1[l, 1], ffn_w3[l, 1], ffn_w2[l, 1])
        if update_ctx:
            hc = hc + mc[:, 5] * y_ctx
            hc = half_ffn(hc, mc[:, 6], mc[:, 7], mc[:, 8], norm_g[l, 2], ffn_w1[l, 1], ffn_w3[l, 1], ffn_w2[l, 1])
    return rms_norm(h, final_g)
```

```python
import math
from contextlib import ExitStack

import numpy as np
import ml_dtypes
import concourse.bass as bass
import concourse.mybir as mybir
from concourse.bass_utils import run_bass_kernel_spmd

F32 = mybir.dt.float32
BF16 = mybir.dt.bfloat16
AF = mybir.ActivationFunctionType
ALU = mybir.AluOpType

D = 1024
SEQ = 4096
CTX = 256
NTOK = SEQ + CTX
DFF = 2816
NJ = DFF // 128
INC = 7936
T = 512
HALO = 16
WT = T + 2 * HALO
NKC = NTOK // 128
EPS = 1e-6
ENGS = ("pe", "act", "dve", "pool", "sp")
NSLOT = 4
SLOT = 2048


class Sched:
    def __init__(self, nc, stack):
        self.nc = nc
        self.stack = stack
        self.q = {e: [] for e in ENGS}
        self.cnt = {e: 0 for e in ENGS}
        self.seen = {e: {} for e in ENGS}
        self.last_w = {}
        self.readers = {}
        self.sems = {}
        self.dcnt = {}
        for e in ("pe", "act", "dve", "pool"):
            self.sems[e] = stack.enter_context(nc.semaphore("s_" + e))

    def _dsem(self, key):
        if key not in self.sems:
            self.sems[key] = self.stack.enter_context(self.nc.semaphore("d_" + key))
            self.dcnt[key] = 0

    def _deps(self, eng, reads, writes):
        toks = set()
        for k in reads:
            w = self.last_w.get(k)
            if w is not None:
                toks.add(w)
        for k in writes:
            w = self.last_w.get(k)
            if w is not None:
                toks.add(w)
            for r in self.readers.get(k, ()):
                toks.add(r)
        waits = {}
        for (s, v) in toks:
            if s == eng:
                if eng == "pe":
                    continue
                if self.cnt[eng] - v >= 1:
                    continue
            if self.seen[eng].get(s, 0) >= v:
                continue
            if waits.get(s, 0) < v:
                waits[s] = v
        for s, v in waits.items():
            self.seen[eng][s] = v
        return list(waits.items())

    def _commit(self, tok, reads, writes):
        for k in writes:
            self.last_w[k] = tok
            self.readers[k] = []
        for k in reads:
            self.readers.setdefault(k, []).append(tok)

    def op(self, eng, fn, reads=(), writes=()):
        waits = self._deps(eng, reads, writes)
        self.cnt[eng] += 1
        tok = (eng, self.cnt[eng])
        self.q[eng].append((waits, fn, (eng, 1)))
        self._commit(tok, reads, writes)
        return tok

    def dma(self, eng, semkey, fn, reads=(), writes=()):
        self._dsem(semkey)
        waits = self._deps(eng, reads, writes)
        self.dcnt[semkey] += 16
        tok = (semkey, self.dcnt[semkey])
        self.q[eng].append((waits, fn, (semkey, 16)))
        self._commit(tok, reads, writes)
        return tok

    def barrier(self, skip=None):
        for e in ENGS:
            waits = []
            for s in self.sems:
                if skip is not None and s.startswith(skip):
                    continue
                v = self.cnt[s] if s in self.cnt else self.dcnt[s]
                if s == e or v == 0 or self.seen[e].get(s, 0) >= v:
                    continue
                waits.append((s, v))
                self.seen[e][s] = v
            if waits:
                self.q[e].append((waits, None, None))

    def emit(self):
        nc = self.nc
        engmap = {"pe": "tensor", "act": "scalar", "dve": "vector", "pool": "gpsimd", "sp": "sync"}
        with nc.Block() as block:
            for e in ENGS:
                items = self.q[e]
                if not items:
                    continue

                def body(engine, items=items):
                    for waits, fn, inc in items:
                        for s, v in waits:
                            engine.wait_ge(self.sems[s], v)
                        if fn is not None:
                            ins = fn(engine)
                            ins.then_inc(self.sems[inc[0]], inc[1])

                getattr(block, engmap[e])(body)


def build(debug=None):
    nc = bass.Bass("TRN2", target_bir_lowering=False)

    def din(name, shape, dt=F32):
        return nc.dram_tensor(name, list(shape), dt, kind="ExternalInput")

    x_d = din("x", [SEQ, D]).ap()
    ctx_d = din("ctx", [CTX, D]).ap()
    cc_d = din("cc", [2, D]).ap()
    ada_w = din("ada_w", [2, D, 9 * D]).ap()
    ada_b = din("ada_b", [2, 9 * D]).ap()
    norm_g = din("norm_g", [2, 3, D]).ap()
    w1_d = din("ffn_w1", [2, 2, D, DFF]).ap()
    w3_d = din("ffn_w3", [2, 2, D, DFF]).ap()
    w2_d = din("ffn_w2", [2, 2, DFF, D]).ap()
    win_d = din("w_in", [2, D, INC]).ap()
    qkn_d = din("qk_norm_a", [2, 2, 64])
    cw_d = din("conv_w", [2, 31, 512]).ap()
    cb_d = din("conv_b", [2, 512]).ap()
    clg_d = din("conv_ln_g", [2, 512]).ap()
    clb_d = din("conv_ln_b", [2, 512]).ap()
    lam_d = din("diff_lam", [2, 4, 64])
    sub_d = din("diff_subln_g", [2, 128])
    wb_d = din("w_branch", [2, 4, 512, D]).ap()
    wo_d = din("w_out", [2, D, D]).ap()
    fg_d = din("final_g", [D]).ap()
    ident_d = din("c_ident", [128, 128]).ap()
    rmat_d = din("c_rmat", [128, 128]).ap()
    bones_d = din("c_bones", [128, 128]).ap()
    rope_d = din("c_rope", [2, 128, SEQ]).ap()
    dftL_d = din("c_dftL", [8, 16, 128, 2048], BF16).ap()
    dftC_d = din("c_dftC", [128, 2 * 512], BF16).ap()
    cmat_d = din("c_cmat", [128, 256], BF16).ap()
    out_d = nc.dram_tensor("out", [SEQ, D], F32, kind="ExternalOutput").ap()

    h1_d = nc.dram_tensor("h1s", [128, 8, NTOK], F32).ap()
    h2_d = nc.dram_tensor("h2s", [128, 8, NTOK], F32).ap()
    up_d = [[nc.dram_tensor(f"up{l}{f}", [NJ, 128, 8, 256], BF16).ap() for f in range(2)] for l in range(2)]
    dn_d = [[nc.dram_tensor(f"dn{l}{f}", [16, 128, 11, 128], BF16).ap() for f in range(2)] for l in range(2)]
    wi_s = [nc.dram_tensor(f"wi{l}", [31, 128, 8, 256], BF16).ap() for l in range(2)]
    wb_s = [nc.dram_tensor(f"wb{l}", [16, 128, 4, 256], BF16).ap() for l in range(2)]
    wo_s = [nc.dram_tensor(f"wo{l}", [4, 128, 8, 256], BF16).ap() for l in range(2)]

    dbg_out = {}

    with ExitStack() as st:
        S = Sched(nc, st)
        st.enter_context(nc.allow_non_contiguous_dma(reason="small parameter vectors"))

        def sb(name, shape, dt):
            return st.enter_context(nc.sbuf_tensor(name, list(shape), dt))

        KaT = sb("KaT", [128, NTOK], BF16)
        Va = sb("Va", [128, NKC, 192], BF16)
        KcT = sb("KcT", [128, 4, NTOK], BF16)
        Vc = sb("Vc", [128, NKC, 512], BF16)
        Zc = sb("Zc", [128, NKC, 512], BF16)
        hT = sb("hT", [128, 8, WT], F32)
        nT = sb("nT", [128, 8, WT], BF16)
        AR = sb("AR", [128, 24 * 512], BF16)
        ARf = AR.bitcast(F32)
        Vcf = Vc.bitcast(F32)
        ring = [sb(f"ring{i}", [128, SLOT], BF16) for i in range(NSLOT)]
        tmp = [sb(f"tmp{i}", [128, WT], F32) for i in range(3)]
        tmpb = [t_.bitcast(BF16) for t_ in tmp]
        rstd = sb("rstd", [128, WT], F32)
        bonesb = sb("bonesb", [128, 128], BF16)
        cosb = sb("cosb", [128, T], F32)
        sinb = sb("sinb", [128, T], F32)
        ident = sb("ident", [128, 128], F32)
        rmat = sb("rmat", [128, 128], F32)
        onesf = sb("onesf", [128, 128], F32)
        onesb = sb("onesb", [128, 128], BF16)
        cmat = sb("cmat", [128, 256], BF16)
        scT = sb("scT", [128, 16], F32)
        modT = sb("modT", [128, 2, 72, 2], F32)
        ngT = sb("ngT", [128, 2, 24], F32)
        fgT = sb("fgT", [128, 8], F32)
        cwT = sb("cwT", [128, 2, 124], F32)
        cpT = sb("cpT", [128, 2, 12], F32)
        qkg = sb("qkg", [128, 2, 2], F32)
        sgs = sb("sgs", [128, 2], F32)
        lsm = sb("lsm", [128, 8], F32)
        nlam = sb("nlam", [128, 2], F32)
        Gm = sb("Gm", [128, 2, 2, 3, 8], F32)
        Gh = sb("Gh", [128, 2, 2, 3, 8], F32)
        epst = sb("epst", [128, 1], F32)
        psall = st.enter_context(nc.psum_tensor("psall", [128, 4096], F32))

        class _Bank:
            def __init__(self, b):
                self.b = b

            def __getitem__(self, idx):
                pp, cc_ = idx
                c0 = cc_.start or 0
                c1 = cc_.stop if cc_.stop is not None else 512
                return psall[pp, self.b * 512 + c0: self.b * 512 + c1]

        ps = [_Bank(i) for i in range(8)]

        def psap(b, off, dims):
            return bass.AP(psall, b * 512 + off, [[4096, 128]] + dims)

        def MM(out, lhsT, rhs, start, stop, reads, writes):
            S.op("pe", lambda e: e.matmul(out, lhsT, rhs, start=start, stop=stop), reads, writes)

        def TR(out, in_, idn, reads, writes):
            S.op("pe", lambda e: e.transpose(out, in_, idn), reads, writes)

        def ACT(out, in_, func, reads, writes, bias=0.0, scale=1.0):
            S.op("act", lambda e: e.activation(out, in_, func, bias=bias, scale=scale), reads, writes)

        def ACOPY(out, in_, reads, writes):
            S.op("act", lambda e: e.copy(out, in_), reads, writes)

        def VCOPY(out, in_, reads, writes, eng="dve"):
            S.op(eng, lambda e: e.tensor_copy(out, in_), reads, writes)

        def TT(out, in0, in1, op, reads, writes, eng="dve"):
            S.op(eng, lambda e: e.tensor_tensor(out, in0, in1, op), reads, writes)

        def TS(out, in0, s1, s2, op0, op1, reads, writes, eng="dve"):
            if s2 is None:
                S.op(eng, lambda e: e.tensor_scalar(out, in0, s1, None, op0), reads, writes)
            else:
                S.op(eng, lambda e: e.tensor_scalar(out, in0, s1, s2, op0, op1), reads, writes)

        def STT(out, in0, scalar, in1, op0, op1, reads, writes, eng="dve"):
            S.op(eng, lambda e: e.scalar_tensor_tensor(out, in0, scalar, in1, op0, op1), reads, writes)

        def RECIP(out, in_, reads, writes):
            S.op("dve", lambda e: e.reciprocal(out, in_), reads, writes)

        def MEMSET(out, val, reads, writes, eng="dve"):
            S.op(eng, lambda e: e.memset(out, val), reads, writes)

        def DMA(eng, sem, out, in_, reads, writes):
            return S.dma(eng, sem, lambda e: e.dma_start(out=out, in_=in_), reads, writes)

        bank_ctr = {}

        def nb(pool=7, base=0):
            k = bank_ctr.get((pool, base), 0)
            bank_ctr[(pool, base)] = k + 1
            return base + (k % pool)

        def PK(b):
            return ("ps", b)

        def pg(a, b=None):
            if b is None:
                b = a + 1
            return [("pg", i) for i in range(a, b)]

        def page(p, n=T):
            return AR[:, p * 512: p * 512 + n]

        tmp_ctr = [0]

        def ntmp():
            tmp_ctr[0] += 1
            return tmp_ctr[0] % 3

        def TK(i):
            return ("tmp", i)

        ring_ctr = [0]

        def wload(src_ap, nelem, rkeys):
            s = ring_ctr[0] % NSLOT
            ring_ctr[0] += 1
            DMA("sp", f"ring{s}", ring[s][:, 0:nelem], src_ap, rkeys, [("ring", s)])
            return ring[s], ("ring", s)

        def dbg(name, ap_sb, shape, rkeys):
            if debug is None or name not in debug:
                return
            t = nc.dram_tensor("dbg_" + name, list(shape), F32, kind="ExternalOutput").ap()
            dbg_out[name] = DMA("sp", "dbg", t, ap_sb, rkeys, [("dbg", name)])

        def cast(sem, dst, src, wkey):
            DMA("pool", sem, dst, src, [], [wkey])

        wkeys = {}

        def convert_ffn(l, f):
            keys = []
            for j in range(NJ):
                for wi_, (wd, off) in enumerate(((w1_d, 0), (w3_d, 128))):
                    k = ("up", l, f, j, wi_)
                    cast(f"cv_up{l}{f}", up_d[l][f][j, :, :, off:off + 128],
                         wd[l, f, :, j * 128:(j + 1) * 128].rearrange("(kc p) m -> p kc m", p=128), k)
                    keys.append(k)
            wkeys[("up", l, f)] = keys
            keys = []
            for d in range(8):
                for hh in range(2):
                    k = ("dn", l, f, d, hh)
                    cast(f"cv_dn{l}{f}", dn_d[l][f][2 * d + hh],
                         w2_d[l, f, hh * 1408:(hh + 1) * 1408, d * 128:(d + 1) * 128].rearrange("(j p) m -> p j m", p=128), k)
                    keys.append(k)
            wkeys[("dn", l, f)] = keys

        def convert_mix(l):
            keys = []
            for u in range(31):
                k = ("wi", l, u)
                keys.append(k)
                if u < 2:
                    for e in range(2):
                        c = 2 * u + e
                        for hh, head in enumerate((c, 4 + c)):
                            kk = ("wi", l, u, e, hh)
                            cast(f"cv_wi{l}", wi_s[l][u, :, :, e * 128 + hh * 64: e * 128 + hh * 64 + 64],
                                 win_d[l, :, head * 64:(head + 1) * 64].rearrange("(kc p) m -> p kc m", p=128), kk)
                            keys.append(kk)
                else:
                    cast(f"cv_wi{l}", wi_s[l][u],
                         win_d[l, :, u * 256:(u + 1) * 256].rearrange("(kc p) m -> p kc m", p=128), k)
            wkeys[("wi", l)] = keys
            keys = []
            for n in range(4):
                for dcp in range(4):
                    if n == 0:
                        for kc in range(4):
                            for hh, head in enumerate((kc, 4 + kc)):
                                k = ("wb", l, n, dcp, kc, hh)
                                cast(f"cv_wb{l}", wb_s[l][n * 4 + dcp, hh * 64:(hh + 1) * 64, kc, :],
                                     wb_d[l, 0, head * 64:(head + 1) * 64, dcp * 256:(dcp + 1) * 256], k)
                                keys.append(k)
                    else:
                        k = ("wb", l, n, dcp)
                        cast(f"cv_wb{l}", wb_s[l][n * 4 + dcp],
                             wb_d[l, n, :, dcp * 256:(dcp + 1) * 256].rearrange("(kc p) m -> p kc m", p=128), k)
                        keys.append(k)
            wkeys[("wb", l)] = keys
            keys = []
            for u in range(4):
                k = ("wo", l, u)
                cast(f"cv_wo{l}", wo_s[l][u],
                     wo_d[l, :, u * 256:(u + 1) * 256].rearrange("(kc p) m -> p kc m", p=128), k)
                keys.append(k)
            wkeys[("wo", l)] = keys

        for l in range(2):
            convert_ffn(l, 0)
            convert_mix(l)
            convert_ffn(l, 1)

        DMA("sp", "c_id", ident[:], ident_d, [], ["ident"])
        DMA("sp", "c_rm", rmat[:], rmat_d, [], ["rmat"])
        DMA("sp", "c_bo", tmp[0][:, 0:128], bones_d, [], [TK(0)])
        DMA("sp", "c_cm", cmat[:], cmat_d, [], ["cmat"])
        MEMSET(onesf[:], 1.0, [], ["onesf"])
        MEMSET(hT[:], 1.0, [], [("hT", c) for c in range(8)])
        MEMSET(epst[:], EPS, [], ["epst"])
        MEMSET(onesb[:], 1.0, [], ["onesb"])
        VCOPY(bonesb[:], tmp[0][:, 0:128], [TK(0)], ["bonesb"])
        MEMSET(Va[:, :, 64:128], 1.0, [], ["Va_ones"])

        stg_ctr = [0]

        def loadT(dst, rows_src_list, R):
            si = stg_ctr[0] % 4
            stg_ctr[0] += 1
            for (r0, nr, src) in rows_src_list:
                DMA("sp", f"stg{si}", rstd[r0:r0 + nr, si * 128:(si + 1) * 128], src, [], [("stage", si)])
            b = nb()
            TR(ps[b][:, 0:R], rstd[0:R, si * 128:(si + 1) * 128], ident[0:R, 0:R], [("stage", si), "ident"], [PK(b)])
            VCOPY(dst, ps[b][:, 0:R], [], [PK(b), "params"])

        loadT(scT[:], [(0, 16, cc_d.rearrange("w (c p) -> (w c) p", p=128))], 16)
        ACT(scT[:], scT[:], AF.Silu, [], ["params", "scT"])
        loadT(fgT[:], [(0, 8, fg_d.rearrange("(c p) -> c p", p=128))], 8)
        for l in range(2):
            loadT(tmp[0][:, 256 + l * 72:256 + (l + 1) * 72], [(0, 72, ada_b[l].rearrange("(r p) -> r p", p=128))], 72)
            loadT(ngT[:, l, :], [(0, 24, norm_g[l].rearrange("j (c p) -> (j c) p", p=128))], 24)
            loadT(cwT[:, l, :], [(0, 124, cw_d[l].rearrange("k (c p) -> (k c) p", p=128))], 124)
            loadT(cpT[:, l, :], [(0, 4, cb_d[l].rearrange("(c p) -> c p", p=128)),
                                 (4, 4, clg_d[l].rearrange("(c p) -> c p", p=128)),
                                 (8, 4, clb_d[l].rearrange("(c p) -> c p", p=128))], 12)
            TS(cwT[:, l, :], cwT[:, l, :], 0.5, None, ALU.mult, None, [], ["params"])
            for hh in range(2):
                DMA("sp", "c1", qkg[hh * 64:(hh + 1) * 64, l, :], bass.AP(qkn_d, l * 128, [[1, 64], [64, 2]]), [], ["params"])
            DMA("sp", "c1", sgs[:, l:l + 1], bass.AP(sub_d, l * 128, [[1, 128], [1, 1]]), [], ["params"])
            DMA("sp", "c2", tmp[1 + l][:, 0:256], bass.AP(lam_d, l * 256, [[0, 128], [1, 256]]), [], [TK(1 + l)])
        S.barrier(skip="cv_")
        for l in range(2):
            lam_init = 0.8 - 0.6 * math.exp(-0.3 * l)
            for q in range(2):
                TT(tmp[0][:, 0:64], tmp[1 + l][:, q * 128:q * 128 + 64], tmp[1 + l][:, q * 128 + 64:q * 128 + 128], ALU.mult,
                   [TK(1 + l)], [TK(0)])
                S.op("dve", lambda e, o=lsm[:, l * 2 + q:l * 2 + q + 1], i=tmp[0][:, 0:64]:
                     e.reduce_sum(o, i, mybir.AxisListType.X), [TK(0)], ["lsm"])
            ACT(lsm[:, l * 2:l * 2 + 2], lsm[:, l * 2:l * 2 + 2], AF.Exp, [], ["lsm"])
            STT(nlam[:, l:l + 1], lsm[:, l * 2 + 1:l * 2 + 2], -lam_init, lsm[:, l * 2:l * 2 + 1], ALU.add, ALU.subtract,
                ["lsm"], ["params"])
            TS(sgs[:, l:l + 1], sgs[:, l:l + 1], 1.0 - lam_init, None, ALU.mult, None, [], ["params"])

        for l in range(2):
            bk = nb()
            for blk in range(18):
                sidx = blk % 2
                DMA("sp", f"ada{sidx}", bass.AP(Vcf, sidx * 4096, [[NKC * 256, 128], [512, 8], [1, 512]]),
                    ada_w[l, :, blk * 512:(blk + 1) * 512].rearrange("(kc p) n -> p kc n", p=128),
                    [], [("adastg", sidx)])
                for mc in range(4):
                    m = blk * 4 + mc
                    for kc in range(8):
                        MM(ps[bk][:, 2 * m:2 * m + 2],
                           bass.AP(Vcf, sidx * 4096 + kc * 512 + mc * 128, [[NKC * 256, 128], [1, 128]]),
                           bass.AP(scT, kc, [[16, 128], [8, 2]]),
                           kc == 0, kc == 7, [("adastg", sidx), "scT", "params"], [PK(bk)])
            for w in range(2):
                TT(modT[:, l, :, w], psap(bk, w, [[2, 72]]), tmp[0][:, 256 + l * 72:256 + (l + 1) * 72], ALU.add,
                   ["params"], [PK(bk), "mod"])
        for l in range(2):
            for w in range(2):
                for idx in range(3):
                    TS(Gm[:, l, w, idx, :], modT[:, l, (3 * idx + 1) * 8:(3 * idx + 2) * 8, w], 1.0, None, ALU.add, None,
                       ["mod"], ["G"])
                    TT(Gm[:, l, w, idx, :], Gm[:, l, w, idx, :], ngT[:, l, idx * 8:(idx + 1) * 8], ALU.mult,
                       ["params", "G"], ["G"])
                    TS(Gh[:, l, w, idx, :], modT[:, l, (3 * idx + 2) * 8:(3 * idx + 3) * 8, w], 0.5, None, ALU.mult, None,
                       ["mod"], ["G"])
        S.barrier(skip="cv_")

        def HK(c):
            return ("hT", c)

        HKall = [("hT", c) for c in range(8)]
        STATB = 7

        def stat_sq(c, nchunks, n, cols=HALO):
            ti = ntmp()
            ACT(tmpb[ti][:, 0:n], hT[:, c, cols:cols + n], AF.Square, [HK(c)], [TK(ti)])

            def mm():
                MM(ps[STATB][:, 0:n], onesb[:], tmpb[ti][:, 0:n], c == 0, c == nchunks - 1, [TK(ti), "onesb"], [PK(STATB)])
            return mm

        def rms_rstd(src_fn, nchunks, cols, n, inv_n, lhs, lhs_key, skeys_fn, halo=False, pre=False):
            bk = STATB if pre else nb()
            bh = nb() if halo else None
            for c in range(nchunks):
                if pre:
                    break
                ti = ntmp()
                if halo:
                    ACT(tmpb[ti][:, 0:WT], src_fn(c, 0, WT), AF.Square, skeys_fn(c), [TK(ti)])
                    MM(ps[bk][:, 0:T], lhs, tmpb[ti][:, HALO:HALO + T], c == 0, c == nchunks - 1, [TK(ti), lhs_key], [PK(bk)])
                    MM(ps[bh][:, 0:32], lhs, bass.AP(tmpb[ti], 0, [[2 * WT, 128], [HALO + T, 2], [1, 16]]),
                       c == 0, c == nchunks - 1, [TK(ti), lhs_key], [PK(bh)])
                else:
                    ACT(tmpb[ti][:, 0:n], src_fn(c, cols, n), AF.Square, skeys_fn(c), [TK(ti)])
                    MM(ps[bk][:, 0:n], lhs, tmpb[ti][:, 0:n], c == 0, c == nchunks - 1, [TK(ti), lhs_key], [PK(bk)])
            if halo:
                ACT(rstd[:, HALO:HALO + T], ps[bk][:, 0:T], AF.Ln, [], [PK(bk), "rstd"], bias=epst[:, 0:1], scale=inv_n)
                ACT(bass.AP(rstd, 0, [[WT, 128], [HALO + T, 2], [1, 16]]), psap(bh, 0, [[16, 2], [1, 16]]), AF.Ln, [], [PK(bh), "rstd"],
                    bias=epst[:, 0:1], scale=inv_n)
                ACT(rstd[:, 0:WT], rstd[:, 0:WT], AF.Exp, [], ["rstd"], scale=-0.5)
            else:
                ACT(rstd[:, cols:cols + n], ps[bk][:, 0:n], AF.Ln, [], [PK(bk), "rstd"], bias=epst[:, 0:1], scale=inv_n)
                ACT(rstd[:, cols:cols + n], rstd[:, cols:cols + n], AF.Exp, [], ["rstd"], scale=-0.5)

        def norm_mod(l, w, idx, cols, n, halo=False, pre=False):
            rms_rstd(lambda c, c0, nn: hT[:, c, c0:c0 + nn], 8, cols, n, 1.0 / D, onesb[:], "onesb", lambda c: [HK(c)], halo, pre)
            c0, nn = (0, WT) if halo else (cols, n)
            for c in range(8):
                ti = ntmp()
                STT(tmp[ti][:, 0:nn], hT[:, c, c0:c0 + nn], Gm[:, l, w, idx, c:c + 1], rstd[:, c0:c0 + nn], ALU.mult, ALU.mult,
                    [HK(c), "rstd", "G"], [TK(ti)])
                ACT(nT[:, c, c0:c0 + nn], tmp[ti][:, 0:nn], AF.Identity, [TK(ti), "mod"], [("nT", c)],
                    bias=modT[:, l, 3 * idx * 8 + c, w:w + 1])

        def ffn(l, f, w, idx, n, pre=False, post=None):
            cols = HALO
            norm_mod(l, w, idx, cols, n, pre=pre)
            nTk = [("nT", c) for c in range(8)]
            for j in range(NJ):
                slot, sk = wload(up_d[l][f][j].rearrange("p kc m -> p (kc m)"), 2048, wkeys[("up", l, f)])
                ba, bb = nb(), nb()
                for kc in range(8):
                    MM(ps[ba][:, 0:n], slot[:, kc * 256:kc * 256 + 128], nT[:, kc, cols:cols + n], kc == 0, kc == 7,
                       [sk] + nTk, [PK(ba), PK(bb)] if kc == 0 else [PK(ba)])
                for kc in range(8):
                    MM(ps[bb][:, 0:n], slot[:, kc * 256 + 128:kc * 256 + 256], nT[:, kc, cols:cols + n], kc == 0, kc == 7,
                       [sk] + nTk, [PK(bb)])
                ti = ntmp()
                ACT(tmp[ti][:, 0:n], ps[ba][:, 0:n], AF.Silu, [], [PK(ba), TK(ti)])
                TT(page(j, n), tmp[ti][:, 0:n], ps[bb][:, 0:n], ALU.mult, [TK(ti)], [PK(bb)] + pg(j))
            pend = [None]
            for d in range(8):
                bo = nb()
                sl2 = [wload(dn_d[l][f][2 * d + hh].rearrange("p j m -> p (j m)"), 1408, wkeys[("dn", l, f)]) for hh in range(2)]
                for hh in range(2):
                    slot, sk = sl2[hh]
                    for jj in range(11):
                        j = hh * 11 + jj
                        MM(ps[bo][:, 0:n], slot[:, jj * 128:(jj + 1) * 128], page(j, n), j == 0, j == NJ - 1,
                           ([sl2[0][1], sl2[1][1]] if j == 0 else [sk]) + pg(j), [PK(bo)])
                    if hh == 0 and pend[0] is not None:
                        pend[0]()
                        pend[0] = None
                STT(hT[:, d, cols:cols + n], ps[bo][:, 0:n], Gh[:, l, w, idx, d:d + 1], hT[:, d, cols:cols + n],
                    ALU.mult, ALU.add, ["G"], [PK(bo), HK(d)])
                if post is not None:
                    pend[0] = post(d)
            if pend[0] is not None:
                pend[0]()

        def load_rope(tile):
            DMA("sp", "ropec", cosb[:], rope_d[0, :, tile * T:(tile + 1) * T], [], ["cosb"])
            DMA("sp", "ropes", sinb[:], rope_d[1, :, tile * T:(tile + 1) * T], [], ["sinb"])

        def qk_post(bk, n, dst, dkeys, gain, rope):
            xi, si, ci = 0, 1, 2
            xf = tmp[xi]
            if gain is not None:
                ACT(tmpb[si][:, 0:n], ps[bk][:, 0:n], AF.Square, [], [PK(bk), TK(si)])
                ACOPY(xf[:, 0:n], ps[bk][:, 0:n], [], [PK(bk), TK(xi)])
                b2 = nb()
                MM(ps[b2][:, 0:n], bonesb[:], tmpb[si][:, 0:n], True, True, [TK(si), "bonesb"], [PK(b2)])
                ACT(tmp[si][:, 0:n], ps[b2][:, 0:n], AF.Ln, [], [PK(b2), TK(si)], bias=epst[:, 0:1], scale=1.0 / 64)
                ACT(tmp[si][:, 0:n], tmp[si][:, 0:n], AF.Exp, [], [TK(si)], scale=-0.5)
                STT(xf[:, 0:n], xf[:, 0:n], gain, tmp[si][:, 0:n], ALU.mult, ALU.mult, [TK(si), "params"], [TK(xi)])
            else:
                ACOPY(xf[:, 0:n], ps[bk][:, 0:n], [], [PK(bk), TK(xi)])
            if rope:
                b3 = nb()
                MM(ps[b3][:, 0:n], rmat[:], xf[:, 0:n], True, True, [TK(xi), "rmat"], [PK(b3)])
                TT(tmp[ci][:, 0:n], xf[:, 0:n], cosb[:, 0:n], ALU.mult, [TK(xi), "cosb"], [TK(ci)])
                TT(tmp[si][:, 0:n], ps[b3][:, 0:n], sinb[:, 0:n], ALU.mult, ["sinb"], [PK(b3), TK(si)])
                TT(dst, tmp[ci][:, 0:n], tmp[si][:, 0:n], ALU.add, [TK(ci), TK(si)], dkeys)
            else:
                VCOPY(dst, xf[:, 0:n], [TK(xi)], dkeys)

        def proj_fm(l, u, e, n, cols=HALO):
            raise NotImplementedError

        nTk_all = [("nT", c) for c in range(8)]

        def load_tile_x(src, tok0, n):
            ns = n // 128
            DMA("sp", "xin", bass.AP(ARf, 0, [[24 * 256, 128], [1024, ns], [1, 1024]]),
                src[tok0:tok0 + n, :].rearrange("(s p) f -> p s f", p=128), [], pg(0, 16))
            pendx = [None]
            for c in range(8):
                b = nb()
                for s_ in range(ns):
                    TR(ps[b][:, s_ * 128:(s_ + 1) * 128], ARf[:, s_ * 1024 + c * 128: s_ * 1024 + (c + 1) * 128], ident[:],
                       pg(0, 16) + ["ident"], [PK(b)])
                if pendx[0] is not None:
                    pendx[0]()
                VCOPY(hT[:, c, HALO:HALO + n], ps[b][:, 0:n], [], [PK(b), HK(c)])
                pendx[0] = stat_sq(c, 8, n)
            pendx[0]()

        def hkey(which, l, t, c):
            return (which, t, c)

        def sweep1(l, tile):
            w = 1 if tile == 8 else 0
            n = CTX if w else T
            tok0 = SEQ if w else tile * T
            kc0 = tok0 // 128
            cols = HALO
            if l == 0:
                load_tile_x(ctx_d if w else x_d, 0 if w else tok0, n)
            else:
                for c in range(8):
                    DMA("sp", f"hin{c}", hT[:, c, cols:cols + n], h2_d[:, c, tok0:tok0 + n], [hkey("h2", 0, tile, c)], [HK(c)])
            if not w:
                load_rope(tile)

            def post1(d):
                if not (l == 1 and w == 1):
                    DMA("act", f"hout{d}", h1_d[:, d, tok0:tok0 + n], hT[:, d, cols:cols + n], [HK(d)], [hkey("h1", l, tile, d)])
                return stat_sq(d, 8, n)

            ffn(l, 0, w, 0, n, pre=(l == 0), post=post1)
            norm_mod(l, w, 1, cols, n, pre=True)
            wk = wkeys[("wi", l)]
            slot, sk = wload(wi_s[l][2].rearrange("p kc m -> p (kc m)"), 2048, wk)
            b = nb()
            for kc in range(8):
                MM(ps[b][:, 0:n], slot[:, kc * 256:kc * 256 + 128], nT[:, kc, cols:cols + n], kc == 0, kc == 7,
                   [sk] + nTk_all, [PK(b)])
            qk_post(b, n, KaT[:, tok0:tok0 + n], [("KaT", tile)], qkg[:, l, 1:2], rope=(w == 0))
            b = nb()
            ns = n // 128
            for s_ in range(ns):
                for kc in range(8):
                    MM(ps[b][:, s_ * 128:(s_ + 1) * 128], nT[:, kc, cols + s_ * 128:cols + (s_ + 1) * 128],
                       slot[:, kc * 256 + 128:kc * 256 + 256], kc == 0, kc == 7, [sk] + nTk_all, [PK(b)])
            VCOPY(Va[:, kc0:kc0 + ns, 0:64], psap(b, 0, [[128, ns], [1, 64]]), ["Va_ones"],
                  [PK(b), ("Va", tile)])
            ACOPY(Va[:, kc0:kc0 + ns, 128:192], psap(b, 64, [[128, ns], [1, 64]]), ["Va_ones"],
                  [PK(b), ("Va", tile)])
            for uu in range(2):
                slot, sk = wload(wi_s[l][9 + uu].rearrange("p kc m -> p (kc m)"), 2048, wk)
                for e in range(2):
                    h = 2 * uu + e
                    b = nb()
                    for kc in range(8):
                        MM(ps[b][:, 0:n], slot[:, kc * 256 + e * 128:kc * 256 + (e + 1) * 128], nT[:, kc, cols:cols + n],
                           kc == 0, kc == 7, [sk] + nTk_all, [PK(b)])
                    qk_post(b, n, KcT[:, h, tok0:tok0 + n], [("KcT", tile, h)], None, rope=(w == 0))
            for (ubase, dst, dk) in ((11, Vc, "Vc"), (13, Zc, "Zc")):
                for uu in range(2):
                    slot, sk = wload(wi_s[l][ubase + uu].rearrange("p kc m -> p (kc m)"), 2048, wk)
                    for s2 in range(0, ns, 2):
                        b = nb()
                        for s_ in range(s2, s2 + 2):
                            for kc in range(8):
                                MM(ps[b][:, (s_ - s2) * 256:(s_ - s2 + 1) * 256],
                                   nT[:, kc, cols + s_ * 128:cols + (s_ + 1) * 128],
                                   slot[:, kc * 256:(kc + 1) * 256], kc == 0, kc == 7, [sk] + nTk_all, [PK(b)])
                        o = dst[:, kc0 + s2:kc0 + s2 + 2, uu * 256:(uu + 1) * 256]
                        i = psap(b, 0, [[256, 2], [1, 256]])
                        if (s2 // 2 + uu) % 2 == 0:
                            VCOPY(o, i, [], [PK(b), (dk, tile, uu)])
                        else:
                            ACOPY(o, i, [], [PK(b), (dk, tile, uu)])

        def all_keys(name, tiles, extra=None):
            ks = []
            for t in tiles:
                if extra is None:
                    ks.append((name, t))
                else:
                    for x in extra:
                        ks.append((name, t, x))
            return ks

        def sweep2(l, tile):
            w = 1 if tile == 8 else 0
            n = CTX if w else T
            tok0 = SEQ if w else tile * T
            cols = HALO
            ktiles = [8] if w else list(range(9))
            kcs = [32, 33] if w else list(range(NKC))
            first = (tile == 0) or w
            last = (tile == 7) or w
            halo = not w
            lo = tok0 - (0 if first else HALO)
            hi = tok0 + n + (0 if last else HALO)
            c_lo = cols - (tok0 - lo)
            wk = wkeys[("wi", l)]
            pre_units = {0: (wload(wi_s[l][3].rearrange("p kc m -> p (kc m)"), 2048, wk),
                             wload(wi_s[l][5].rearrange("p kc m -> p (kc m)"), 2048, wk))}
            for c in range(8):
                rk = [hkey("h1", l, t, c) for t in ([8] if w else range(max(0, tile - 1), min(7, tile + 1) + 1))]
                DMA("sp", f"hin{c}", hT[:, c, c_lo:c_lo + (hi - lo)], h1_d[:, c, lo:hi], rk, [HK(c)])
            if not w:
                load_rope(tile)
            norm_mod(l, w, 1, cols, n, halo=halo)
            wk = wkeys[("wi", l)]
            GL0 = 0
            AC0 = 4096
            glk = pg(0, 9)
            ack = pg(16, 24)

            def glu(cc, a, b_=None):
                return ARf[:, GL0 + cc * WT + a: GL0 + cc * WT + (b_ if b_ is not None else a)]

            for uu in range(2):
                if uu in pre_units:
                    (sa, ska), (sg_, skg) = pre_units[uu]
                else:
                    sa, ska = wload(wi_s[l][3 + uu].rearrange("p kc m -> p (kc m)"), 2048, wk)
                    sg_, skg = wload(wi_s[l][5 + uu].rearrange("p kc m -> p (kc m)"), 2048, wk)
                for e in range(2):
                    cc = 2 * uu + e
                    ba, bg = nb(), nb()
                    for (bx, sl, skx) in ((ba, sa, ska), (bg, sg_, skg)):
                        for kc in range(8):
                            MM(ps[bx][:, 0:n], sl[:, kc * 256 + e * 128:kc * 256 + (e + 1) * 128], nT[:, kc, cols:cols + n],
                               kc == 0, kc == 7, [skx] + nTk_all, [PK(bx)])
                    ti = ntmp()
                    ACT(tmp[ti][:, 0:n], ps[bg][:, 0:n], AF.Tanh, [], [PK(bg), TK(ti)], scale=0.5)
                    STT(glu(cc, cols, cols + n), tmp[ti][:, 0:n], 1.0, ps[ba][:, 0:n], ALU.add, ALU.mult, [TK(ti)],
                        [PK(ba)] + glk)
                    if halo:
                        bh = nb()
                        for (c0h, sl, skx) in ((0, sa, ska), (32, sg_, skg)):
                            for kc in range(8):
                                MM(ps[bh][:, c0h:c0h + 30], sl[:, kc * 256 + e * 128:kc * 256 + (e + 1) * 128],
                                   bass.AP(nT, kc * WT + 1, [[8 * WT, 128], [HALO + T - 1, 2], [1, 15]]),
                                   kc == 0, kc == 7, [skx] + nTk_all, [PK(bh)])
                        ti = ntmp()
                        ACT(tmp[ti][:, 0:30], ps[bh][:, 32:62], AF.Tanh, [], [PK(bh), TK(ti)], scale=0.5)
                        STT(bass.AP(ARf, GL0 + cc * WT + 1, [[24 * 256, 128], [HALO + T - 1, 2], [1, 15]]),
                            bass.AP(tmp[ti], 0, [[WT, 128], [15, 2], [1, 15]]), 1.0,
                            psap(bh, 0, [[15, 2], [1, 15]]), ALU.add, ALU.mult, [TK(ti)],
                            [PK(bh)] + glk)
                    if first:
                        MEMSET(glu(cc, 0, HALO), 0.0, [], glk)
                    if last:
                        MEMSET(glu(cc, cols + n, cols + n + HALO), 0.0, [], glk)
            for k in range(31):
                for cc in range(4):
                    acc = ARf[:, AC0 + cc * 512: AC0 + cc * 512 + n]
                    gk = pg((cc * WT) // 256, ((cc + 1) * WT + 255) // 256)
                    ak = pg(16 + 2 * cc, 18 + 2 * cc)
                    if k == 0:
                        TS(acc, glu(cc, 1, 1 + n), cwT[:, l, cc:cc + 1], cpT[:, l, cc:cc + 1], ALU.mult, ALU.add,
                           gk + ["params"], ak)
                    else:
                        STT(acc, glu(cc, 1 + k, 1 + k + n), cwT[:, l, k * 4 + cc:k * 4 + cc + 1], acc, ALU.mult, ALU.add,
                            gk + ["params"], ak)
            zk = all_keys("Zc", ktiles[:-1] if not w else ktiles, (0, 1))
            nunits = 1 if w else 16
            for u in range(nunits):
                if w:
                    slot, sk = wload(dftC_d, 1024, [])
                else:
                    slot, sk = wload(dftL_d[tile, u], 2048, [])
                for tcl in range(2):
                    tc = (32 + tcl) if w else (2 * u + tcl)
                    off = tcl * (512 if w else 1024)
                    for g in range(4):
                        for q in range(2):
                            MM(ps[2 * g + q][:, 0:n], Zc[:, tc, g * 128:(g + 1) * 128],
                               slot[:, off + q * n: off + q * n + n],
                               (u == 0 and tcl == 0), (u == nunits - 1 and tcl == 1), [sk] + zk, [PK(2 * g + q)])
            for gp in range(2):
                for gg in range(2):
                    g = 2 * gp + gg
                    for q in range(2):
                        ACOPY(tmpb[gg][:, q * WT:q * WT + n], ps[2 * g + q][:, 0:n], [], [PK(2 * g + q), TK(gg)])
                for gg in range(2):
                    g = 2 * gp + gg
                    b = 2 * g
                    MM(ps[b][:, 0:n], cmat[:, 0:128], tmpb[gg][:, 0:n], True, False, ["cmat", TK(gg)], [PK(b)])
                    MM(ps[b][:, 0:n], cmat[:, 128:256], tmpb[gg][:, WT:WT + n], False, True, ["cmat", TK(gg)], [PK(b)])
                    ACOPY(page(9 + g, n), ps[b][:, 0:n], [], [PK(b)] + pg(9 + g))
            bm = nb()
            for cc in range(4):
                MM(ps[bm][:, 0:n], onesf[:], ARf[:, AC0 + cc * 512: AC0 + cc * 512 + n], cc == 0, cc == 3,
                   ack + ["onesf"], [PK(bm)])
            for cc in range(4):
                acc = ARf[:, AC0 + cc * 512: AC0 + cc * 512 + n]
                STT(acc, ps[bm][:, 0:n], -1.0 / 512, acc, ALU.mult, ALU.add, [], [PK(bm)] + ack)
            rms_rstd(lambda c, c0, nn: ARf[:, AC0 + c * 512: AC0 + c * 512 + nn], 4, cols, n, 1.0 / 512, onesb[:], "onesb", lambda c: ack)
            for cc in range(4):
                acc = ARf[:, AC0 + cc * 512: AC0 + cc * 512 + n]
                TT(acc, acc, rstd[:, cols:cols + n], ALU.mult, ["rstd"], ack)
                ACT(page(cc, n), acc, AF.Silu, ack + ["params"], pg(cc), bias=cpT[:, l, 8 + cc:9 + cc],
                    scale=cpT[:, l, 4 + cc:5 + cc])
            kak = all_keys("KaT", ktiles)
            vak = all_keys("Va", ktiles) + ["Va_ones"]
            for uu in range(2):
                slot, sk = wload(wi_s[l][uu].rearrange("p kc m -> p (kc m)"), 2048, wk)
                for e in range(2):
                    c = 2 * uu + e
                    b = nb()
                    for kc in range(8):
                        MM(ps[b][:, 0:n], slot[:, kc * 256 + e * 128:kc * 256 + (e + 1) * 128], nT[:, kc, cols:cols + n],
                           kc == 0, kc == 7, [sk] + nTk_all, [PK(b)])
                    qk_post(b, n, page(16 + c, n), pg(16 + c), qkg[:, l, 0:1], rope=(w == 0))
            SC = 0.125

            def groups_of(G):
                return [kcs[i:i + G] for i in range(0, len(kcs), G)]

            def exp_group(sb0, ep0, g):
                ACT(bass.AP(AR, ep0 * 512, [[24 * 512, 128], [512, g], [1, n]]), psap(sb0, 0, [[512, g], [1, n]]), AF.Exp, [],
                    [PK(sb0 + j) for j in range(g)] + pg(ep0, ep0 + g), scale=SC)

            grpsA = groups_of(3)
            for c in range(4):
                for hh in range(2):
                    p0 = hh * 64
                    ba = 6 + nb(2, 100) - 100

                    def issue_sA(gi):
                        sb0 = (gi % 2) * 3
                        for j, kc in enumerate(grpsA[gi]):
                            MM(ps[sb0 + j][:, 0:n], KaT[p0:p0 + 64, kc * 128:(kc + 1) * 128], page(16 + c, n)[p0:p0 + 64, :],
                               True, True, kak + pg(16 + c), [PK(sb0 + j)])

                    issue_sA(0)
                    for gi, grp in enumerate(grpsA):
                        if gi + 1 < len(grpsA):
                            issue_sA(gi + 1)
                        sb0 = (gi % 2) * 3
                        ep0 = (13, 20)[gi % 2]
                        exp_group(sb0, ep0, len(grp))
                        for j, kc in enumerate(grp):
                            MM(ps[ba][:, 0:n], Va[:, kc, hh * 64:hh * 64 + 128], page(ep0 + j, n),
                               gi == 0 and j == 0, gi == len(grpsA) - 1 and j == len(grp) - 1,
                               vak + pg(ep0 + j), [PK(ba)])
                    ti = ntmp()
                    q0 = 64 - p0
                    RECIP(tmp[ti][q0:q0 + 64, 0:n], ps[ba][q0:q0 + 64, 0:n], [], [PK(ba), TK(ti)])
                    TT(page(4 + c, n)[p0:p0 + 64, :], ps[ba][p0:p0 + 64, 0:n], tmp[ti][q0:q0 + 64, 0:n], ALU.mult,
                       [TK(ti)], [PK(ba)] + pg(4 + c))
            kck = all_keys("KcT", ktiles, range(4))
            vck = all_keys("Vc", ktiles, (0, 1))
            for uu in range(2):
                slot, sk = wload(wi_s[l][7 + uu].rearrange("p kc m -> p (kc m)"), 2048, wk)
                for e in range(2):
                    h = 2 * uu + e
                    b = nb()
                    for kc in range(8):
                        MM(ps[b][:, 0:n], slot[:, kc * 256 + e * 128:kc * 256 + (e + 1) * 128], nT[:, kc, cols:cols + n],
                           kc == 0, kc == 7, [sk] + nTk_all, [PK(b)])
                    qk_post(b, n, page(16 + h, n), pg(16 + h), None, rope=(w == 0))
            grpsC = groups_of(2)
            pending = [None]

            def make_final(h):
                def fin():
                    RECIP(tmp[1][:, 0:n], tmp[1][:, 0:n], [], [TK(1)])
                    TT(tmp[0][:, 0:n], tmp[0][:, 0:n], tmp[1][:, 0:n], ALU.mult, [TK(1)], [TK(0)])
                    RECIP(rstd[:, 0:n], rstd[:, 0:n], [], ["rstd"])
                    TT(tmp[2][:, 0:n], tmp[2][:, 0:n], rstd[:, 0:n], ALU.mult, ["rstd"], [TK(2)])
                    STT(tmp[0][:, 0:n], tmp[2][:, 0:n], nlam[:, l:l + 1], tmp[0][:, 0:n], ALU.mult, ALU.add, [TK(2), "params"], [TK(0)])
                    ACT(tmpb[1][:, 0:n], tmp[0][:, 0:n], AF.Square, [TK(0)], [TK(1)])
                    b2 = nb(4)
                    MM(ps[b2][:, 0:n], onesb[:], tmpb[1][:, 0:n], True, True, [TK(1), "onesb"], [PK(b2)])
                    ACT(tmp[1][:, 0:n], ps[b2][:, 0:n], AF.Ln, [], [PK(b2), TK(1)], bias=epst[:, 0:1], scale=1.0 / 128)
                    ACT(tmp[1][:, 0:n], tmp[1][:, 0:n], AF.Exp, [], [TK(1)], scale=-0.5)
                    ych = (13, 14, 15, 8)[h]
                    STT(page(ych, n), tmp[0][:, 0:n], sgs[:, l:l + 1], tmp[1][:, 0:n], ALU.mult, ALU.mult,
                        [TK(0), TK(1), "params"], pg(ych))
                return fin

            for h in range(4):
                for p_ in range(2):
                    p0 = p_ * 64
                    bev, bs_ = 4 + 2 * p_, 5 + 2 * p_

                    def issue_sC(gi):
                        sb0 = (gi % 2) * 2
                        for j, kc in enumerate(grpsC[gi]):
                            MM(ps[sb0 + j][:, 0:n], KcT[p0:p0 + 64, h, kc * 128:(kc + 1) * 128], page(16 + h, n)[p0:p0 + 64, :],
                               True, True, kck + pg(16 + h), [PK(sb0 + j)])

                    issue_sC(0)
                    for gi, grp in enumerate(grpsC):
                        if gi + 1 < len(grpsC):
                            issue_sC(gi + 1)
                        sb0 = (gi % 2) * 2
                        ep0 = (20, 22)[gi % 2]
                        exp_group(sb0, ep0, len(grp))
                        for j, kc in enumerate(grp):
                            first = (gi == 0 and j == 0)
                            lastf = (gi == len(grpsC) - 1 and j == len(grp) - 1)
                            MM(ps[bev][:, 0:n], Vc[:, kc, h * 128:(h + 1) * 128], page(ep0 + j, n), first, lastf,
                               vck + pg(ep0 + j), [PK(bev)])
                        for j, kc in enumerate(grp):
                            first = (gi == 0 and j == 0)
                            lastf = (gi == len(grpsC) - 1 and j == len(grp) - 1)
                            MM(ps[bs_][:, 0:n], onesb[:], page(ep0 + j, n), first, lastf, ["onesb"] + pg(ep0 + j), [PK(bs_)])
                    if p_ == 0 and pending[0] is not None:
                        pending[0]()
                        pending[0] = None
                ACOPY(tmp[0][:, 0:n], ps[4][:, 0:n], [], [PK(4), TK(0)])
                VCOPY(tmp[1][:, 0:n], ps[5][:, 0:n], [], [PK(5), TK(1)])
                ACOPY(tmp[2][:, 0:n], ps[6][:, 0:n], [], [PK(6), TK(2)])
                VCOPY(rstd[:, 0:n], ps[7][:, 0:n], [], [PK(7), "rstd"])
                pending[0] = make_final(h)
            pending[0]()
            ypages = ((4, 5, 6, 7), (0, 1, 2, 3), (13, 14, 15, 8), (9, 10, 11, 12))
            for dcp in range(4):
                for nbr in range(4):
                    sgl, skgl = wload(wi_s[l][15 + nbr * 4 + dcp].rearrange("p kc m -> p (kc m)"), 2048, wk)
                    swb, skwb = wload(wb_s[l][nbr * 4 + dcp].rearrange("p kc m -> p (kc m)"), 1024, wkeys[("wb", l)])
                    bgs = [nb(), nb()]
                    bps = [nb(), nb()]
                    for e in range(2):
                        for kc in range(8):
                            claim = (e == 0 and kc == 0)
                            MM(ps[bgs[e]][:, 0:n], sgl[:, kc * 256 + e * 128:kc * 256 + (e + 1) * 128], nT[:, kc, cols:cols + n],
                               kc == 0, kc == 7, [skgl] + ([skwb] if claim else []) + nTk_all,
                               [PK(bgs[0]), PK(bgs[1]), PK(bps[0]), PK(bps[1])] if claim else [PK(bgs[e])])
                    for e in range(2):
                        for kc in range(4):
                            MM(ps[bps[e]][:, 0:n], swb[:, kc * 256 + e * 128:kc * 256 + (e + 1) * 128], page(ypages[nbr][kc], n),
                               kc == 0, kc == 3, [skwb] + pg(ypages[nbr][kc]), [PK(bps[e])])
                    for e in range(2):
                        dc = 2 * dcp + e
                        bp, bg = bps[e], bgs[e]
                        ACT(tmp[2][:, 0:n], ps[bg][:, 0:n], AF.Tanh, [], [PK(bg), TK(2)], scale=0.5)
                        if nbr == 0:
                            STT(tmp[e][:, 0:n], tmp[2][:, 0:n], 1.0, ps[bp][:, 0:n], ALU.add, ALU.mult, [TK(2)], [PK(bp), TK(e)])
                        else:
                            STT(tmp[2][:, 0:n], tmp[2][:, 0:n], 1.0, ps[bp][:, 0:n], ALU.add, ALU.mult, [], [PK(bp), TK(2)])
                            if nbr < 3:
                                TT(tmp[e][:, 0:n], tmp[e][:, 0:n], tmp[2][:, 0:n], ALU.add, [TK(2)], [TK(e)])
                            else:
                                TT(page(16 + dc, n), tmp[e][:, 0:n], tmp[2][:, 0:n], ALU.add, [TK(2), TK(e)], pg(16 + dc))
            pendw = [None]
            for d in range(8):
                if d % 2 == 0:
                    slot, sk = wload(wo_s[l][d // 2].rearrange("p kc m -> p (kc m)"), 2048, wkeys[("wo", l)])
                bo = nb()
                for kc in range(8):
                    MM(ps[bo][:, 0:n], slot[:, kc * 256 + (d % 2) * 128:kc * 256 + (d % 2 + 1) * 128], page(16 + kc, n),
                       kc == 0, kc == 7, [sk] + pg(16 + kc), [PK(bo)])
                    if kc == 5 and pendw[0] is not None:
                        pendw[0]()
                        pendw[0] = None
                STT(hT[:, d, cols:cols + n], ps[bo][:, 0:n], Gh[:, l, w, 1, d:d + 1], hT[:, d, cols:cols + n],
                    ALU.mult, ALU.add, ["G"], [PK(bo), HK(d)])
                pendw[0] = stat_sq(d, 8, n)
            pendw[0]()

            def post2(d):
                if l == 0:
                    DMA("act", f"hout{d}", h2_d[:, d, tok0:tok0 + n], hT[:, d, cols:cols + n], [HK(d)], [hkey("h2", 0, tile, d)])
                    return None
                return stat_sq(d, 8, n)

            ffn(l, 1, w, 2, n, pre=True, post=post2)
            if l == 1:
                rms_rstd(lambda c, c0, nn: hT[:, c, c0:c0 + nn], 8, cols, n, 1.0 / D, onesb[:], "onesb", lambda c: [HK(c)], pre=True)
                for c in range(8):
                    STT(hT[:, c, cols:cols + n], hT[:, c, cols:cols + n], fgT[:, c:c + 1], rstd[:, cols:cols + n],
                        ALU.mult, ALU.mult, ["rstd", "params"], [HK(c)])
                for s_ in range(4):
                    for cg in range(2):
                        b = nb()
                        for q in range(4):
                            c = cg * 4 + q
                            TR(ps[b][:, q * 128:(q + 1) * 128], hT[:, c, cols + s_ * 128:cols + (s_ + 1) * 128], ident[:],
                               [HK(c), "ident"], [PK(b)])
                        o = ARf[:, s_ * 1024 + cg * 512: s_ * 1024 + (cg + 1) * 512]
                        if cg == 0:
                            VCOPY(o, ps[b][:, 0:512], [], [PK(b)] + pg(0, 16))
                        else:
                            ACOPY(o, ps[b][:, 0:512], [], [PK(b)] + pg(0, 16))
                DMA("sp", "oout", out_d[tok0:tok0 + n, :].rearrange("(s p) f -> p s f", p=128),
                    bass.AP(ARf, 0, [[24 * 256, 128], [1024, 4], [1, 1024]]),
                    pg(0, 16), [("out", tile)])

        for l in range(2):
            for tile in (8, 0, 1, 2, 3, 4, 5, 6, 7):
                sweep1(l, tile)
            tiles2 = list(range(8)) + ([8] if l == 0 else [])
            for tile in tiles2:
                sweep2(l, tile)
        S.barrier()
        print('instr counts', {e: len(v) for e, v in S.q.items()}, flush=True)
        S.emit()
    return nc, list(dbg_out.keys())


_CONST = {}


def _consts():
    if _CONST:
        return _CONST
    ident = np.eye(128, dtype=np.float32)
    rmat = np.zeros((128, 128), np.float32)
    for m in range(128):
        d = m % 64
        q = d // 16
        if q % 2 == 0:
            rmat[m + 16, m] = -1.0
        else:
            rmat[m - 16, m] = 1.0
    bones = np.zeros((128, 128), np.float32)
    bones[:64, :64] = 1.0
    bones[64:, 64:] = 1.0
    t = np.arange(SEQ)
    row = (t // 64).astype(np.float64)
    col = (t % 64).astype(np.float64)
    inv = 10000.0 ** (-np.arange(0, 32, 2, dtype=np.float64) / 32)
    ang = np.concatenate([row[:, None] * inv, row[:, None] * inv, col[:, None] * inv, col[:, None] * inv], axis=1)
    angT = np.concatenate([ang.T, ang.T], axis=0)
    rope = np.stack([np.cos(angT), np.sin(angT)]).astype(np.float32)
    tt = np.arange(SEQ, dtype=np.int64)
    prod = (tt[:, None] * tt[None, :]) % SEQ
    a = 2 * np.pi * prod / SEQ
    Ct = (np.cos(a) / 64.0)
    St = (np.sin(a) / 64.0)
    dftL = np.zeros((8, 16, 128, 2, 1024), dtype=ml_dtypes.bfloat16)
    for tile in range(8):
        cs = Ct[:, tile * T:(tile + 1) * T].reshape(16, 2, 128, T)
        ss = St[:, tile * T:(tile + 1) * T].reshape(16, 2, 128, T)
        blk = np.concatenate([cs, ss], axis=-1)
        dftL[tile] = blk.transpose(0, 2, 1, 3).astype(ml_dtypes.bfloat16)
    dftL = dftL.reshape(8, 16, 128, 2048)
    tc_ = np.arange(CTX, dtype=np.int64)
    ac = 2 * np.pi * ((tc_[:, None] * tc_[None, :]) % CTX) / CTX
    Cc_t = np.cos(ac) / 16.0
    Sc_t = np.sin(ac) / 16.0
    blk = np.concatenate([Cc_t.reshape(2, 128, CTX), Sc_t.reshape(2, 128, CTX)], axis=-1)
    dftC = blk.transpose(1, 0, 2).reshape(128, 1024).astype(ml_dtypes.bfloat16)
    ch = np.arange(128, dtype=np.int64)
    ach = 2 * np.pi * ((ch[:, None] * ch[None, :]) % 128) / 128
    cmat = np.concatenate([np.cos(ach) / math.sqrt(128.0), -np.sin(ach) / math.sqrt(128.0)], axis=1).astype(ml_dtypes.bfloat16)
    _CONST.update(c_ident=ident, c_rmat=rmat, c_bones=bones, c_rope=rope, c_dftL=dftL, c_dftC=dftC, c_cmat=cmat)
    return _CONST


def make_in_maps(inp):
    cst = _consts()
    shared = {k: np.ascontiguousarray(np.asarray(inp[k], dtype=np.float32)) for k in
              ("ada_w", "ada_b", "norm_g", "ffn_w1", "ffn_w3", "ffn_w2", "w_in", "qk_norm_a", "conv_w", "conv_b",
               "conv_ln_g", "conv_ln_b", "diff_lam", "diff_subln_g", "w_branch", "w_out", "final_g")}
    x = np.asarray(inp["x"], dtype=np.float32)
    c = np.asarray(inp["c"], dtype=np.float32)
    ctx = np.asarray(inp["ctx"], dtype=np.float32)
    c_ctx = np.asarray(inp["c_ctx"], dtype=np.float32)
    maps = []
    for b in range(8):
        m = dict(shared)
        m.update(cst)
        m["x"] = np.ascontiguousarray(x[b])
        m["ctx"] = np.ascontiguousarray(ctx[b])
        m["cc"] = np.ascontiguousarray(np.stack([c[b], c_ctx]))
        maps.append(m)
    return maps


_NC = {}


def kernel(**inputs):
    if "nc" not in _NC:
        _NC["nc"] = build()[0]
    nc = _NC["nc"]
    maps = make_in_maps(inputs)
    res = run_bass_kernel_spmd(nc, maps, core_ids=list(range(8)))
    out = np.stack([np.asarray(r["out"], dtype=np.float32) for r in res.results], axis=0)
    return out
```

```python
import math
from contextlib import ExitStack

import numpy as np
import ml_dtypes
import concourse.bass as bass
import concourse.mybir as mybir
from concourse.bass_utils import run_bass_kernel_spmd

F32 = mybir.dt.float32
BF16 = mybir.dt.bfloat16
AF = mybir.ActivationFunctionType
ALU = mybir.AluOpType

D = 1024
SEQ = 4096
CTX = 256
NTOK = SEQ + CTX
DFF = 2816
NJ = DFF // 128
INC = 7936
T = 512
HALO = 16
WT = T + 2 * HALO
NKC = NTOK // 128
EPS = 1e-6
ENGS = ("pe", "act", "dve", "pool", "sp")
NSLOT = 4
SLOT = 2048


class Sched:
    def __init__(self, nc, stack):
        self.nc = nc
        self.stack = stack
        self.q = {e: [] for e in ENGS}
        self.cnt = {e: 0 for e in ENGS}
        self.seen = {e: {} for e in ENGS}
        self.last_w = {}
        self.readers = {}
        self.sems = {}
        self.dcnt = {}
        for e in ("pe", "act", "dve", "pool"):
            self.sems[e] = stack.enter_context(nc.semaphore("s_" + e))

    def _dsem(self, key):
        if key not in self.sems:
            self.sems[key] = self.stack.enter_context(self.nc.semaphore("d_" + key))
            self.dcnt[key] = 0

    def _deps(self, eng, reads, writes):
        toks = set()
        for k in reads:
            w = self.last_w.get(k)
            if w is not None:
                toks.add(w)
        for k in writes:
            w = self.last_w.get(k)
            if w is not None:
                toks.add(w)
            for r in self.readers.get(k, ()):
                toks.add(r)
        waits = {}
        for (s, v) in toks:
            if s == eng:
                if eng == "pe":
                    continue
                if self.cnt[eng] - v >= 1:
                    continue
            if self.seen[eng].get(s, 0) >= v:
                continue
            if waits.get(s, 0) < v:
                waits[s] = v
        for s, v in waits.items():
            self.seen[eng][s] = v
        return list(waits.items())

    def _commit(self, tok, reads, writes):
        for k in writes:
            self.last_w[k] = tok
            self.readers[k] = []
        for k in reads:
            self.readers.setdefault(k, []).append(tok)

    def op(self, eng, fn, reads=(), writes=()):
        waits = self._deps(eng, reads, writes)
        self.cnt[eng] += 1
        tok = (eng, self.cnt[eng])
        self.q[eng].append((waits, fn, (eng, 1)))
        self._commit(tok, reads, writes)
        return tok

    def dma(self, eng, semkey, fn, reads=(), writes=()):
        self._dsem(semkey)
        waits = self._deps(eng, reads, writes)
        self.dcnt[semkey] += 16
        tok = (semkey, self.dcnt[semkey])
        self.q[eng].append((waits, fn, (semkey, 16)))
        self._commit(tok, reads, writes)
        return tok

    def barrier(self, skip=None):
        for e in ENGS:
            waits = []
            for s in self.sems:
                if skip is not None and s.startswith(skip):
                    continue
                v = self.cnt[s] if s in self.cnt else self.dcnt[s]
                if s == e or v == 0 or self.seen[e].get(s, 0) >= v:
                    continue
                waits.append((s, v))
                self.seen[e][s] = v
            if waits:
                self.q[e].append((waits, None, None))

    def emit(self):
        nc = self.nc
        engmap = {"pe": "tensor", "act": "scalar", "dve": "vector", "pool": "gpsimd", "sp": "sync"}
        with nc.Block() as block:
            for e in ENGS:
                items = self.q[e]
                if not items:
                    continue

                def body(engine, items=items):
                    for waits, fn, inc in items:
                        for s, v in waits:
                            engine.wait_ge(self.sems[s], v)
                        if fn is not None:
                            ins = fn(engine)
                            ins.then_inc(self.sems[inc[0]], inc[1])

                getattr(block, engmap[e])(body)


def build(debug=None):
    nc = bass.Bass("TRN2", target_bir_lowering=False)

    def din(name, shape, dt=F32):
        return nc.dram_tensor(name, list(shape), dt, kind="ExternalInput")

    x_d = din("x", [SEQ, D]).ap()
    ctx_d = din("ctx", [CTX, D]).ap()
    cc_d = din("cc", [2, D]).ap()
    ada_w = din("ada_w", [2, D, 9 * D]).ap()
    ada_b = din("ada_b", [2, 9 * D]).ap()
    norm_g = din("norm_g", [2, 3, D]).ap()
    w1_d = din("ffn_w1", [2, 2, D, DFF]).ap()
    w3_d = din("ffn_w3", [2, 2, D, DFF]).ap()
    w2_d = din("ffn_w2", [2, 2, DFF, D]).ap()
    win_d = din("w_in", [2, D, INC]).ap()
    qkn_d = din("qk_norm_a", [2, 2, 64])
    cw_d = din("conv_w", [2, 31, 512]).ap()
    cb_d = din("conv_b", [2, 512]).ap()
    clg_d = din("conv_ln_g", [2, 512]).ap()
    clb_d = din("conv_ln_b", [2, 512]).ap()
    lam_d = din("diff_lam", [2, 4, 64])
    sub_d = din("diff_subln_g", [2, 128])
    wb_d = din("w_branch", [2, 4, 512, D]).ap()
    wo_d = din("w_out", [2, D, D]).ap()
    fg_d = din("final_g", [D]).ap()
    ident_d = din("c_ident", [128, 128]).ap()
    rmat_d = din("c_rmat", [128, 128]).ap()
    bones_d = din("c_bones", [128, 128]).ap()
    rope_d = din("c_rope", [2, 128, SEQ]).ap()
    dftL_d = din("c_dftL", [8, 16, 128, 2048], BF16).ap()
    dftC_d = din("c_dftC", [128, 2 * 512], BF16).ap()
    cmat_d = din("c_cmat", [128, 256], BF16).ap()
    out_d = nc.dram_tensor("out", [SEQ, D], F32, kind="ExternalOutput").ap()

    h1_d = nc.dram_tensor("h1s", [128, 8, NTOK], F32).ap()
    h2_d = nc.dram_tensor("h2s", [128, 8, NTOK], F32).ap()
    up_d = [[nc.dram_tensor(f"up{l}{f}", [NJ, 128, 8, 256], BF16).ap() for f in range(2)] for l in range(2)]
    dn_d = [[nc.dram_tensor(f"dn{l}{f}", [16, 128, 11, 128], BF16).ap() for f in range(2)] for l in range(2)]
    wi_s = [nc.dram_tensor(f"wi{l}", [31, 128, 8, 256], BF16).ap() for l in range(2)]
    wb_s = [nc.dram_tensor(f"wb{l}", [16, 128, 4, 256], BF16).ap() for l in range(2)]
    wo_s = [nc.dram_tensor(f"wo{l}", [4, 128, 8, 256], BF16).ap() for l in range(2)]

    dbg_out = {}

    with ExitStack() as st:
        S = Sched(nc, st)
        st.enter_context(nc.allow_non_contiguous_dma(reason="small parameter vectors"))

        def sb(name, shape, dt):
            return st.enter_context(nc.sbuf_tensor(name, list(shape), dt))

        KaT = sb("KaT", [128, NTOK], BF16)
        Va = sb("Va", [128, NKC, 192], BF16)
        KcT = sb("KcT", [128, 4, NTOK], BF16)
        Vc = sb("Vc", [128, NKC, 512], BF16)
        Zc = sb("Zc", [128, NKC, 512], BF16)
        hT = sb("hT", [128, 8, WT], F32)
        nT = sb("nT", [128, 8, WT], BF16)
        AR = sb("AR", [128, 24 * 512], BF16)
        ARf = AR.bitcast(F32)
        Vcf = Vc.bitcast(F32)
        ring = [sb(f"ring{i}", [128, SLOT], BF16) for i in range(NSLOT)]
        tmp = [sb(f"tmp{i}", [128, WT], F32) for i in range(3)]
        tmpb = [t_.bitcast(BF16) for t_ in tmp]
        rstd = sb("rstd", [128, WT], F32)
        bonesb = sb("bonesb", [128, 128], BF16)
        cosb = sb("cosb", [128, T], F32)
        sinb = sb("sinb", [128, T], F32)
        ident = sb("ident", [128, 128], F32)
        rmat = sb("rmat", [128, 128], F32)
        onesf = sb("onesf", [128, 128], F32)
        onesb = sb("onesb", [128, 128], BF16)
        cmat = sb("cmat", [128, 256], BF16)
        scT = sb("scT", [128, 16], F32)
        modT = sb("modT", [128, 2, 72, 2], F32)
        ngT = sb("ngT", [128, 2, 24], F32)
        fgT = sb("fgT", [128, 8], F32)
        cwT = sb("cwT", [128, 2, 124], F32)
        cpT = sb("cpT", [128, 2, 12], F32)
        qkg = sb("qkg", [128, 2, 2], F32)
        sgs = sb("sgs", [128, 2], F32)
        lsm = sb("lsm", [128, 8], F32)
        nlam = sb("nlam", [128, 2], F32)
        Gm = sb("Gm", [128, 2, 2, 3, 8], F32)
        Gh = sb("Gh", [128, 2, 2, 3, 8], F32)
        epst = sb("epst", [128, 1], F32)
        psall = st.enter_context(nc.psum_tensor("psall", [128, 4096], F32))

        class _Bank:
            def __init__(self, b):
                self.b = b

            def __getitem__(self, idx):
                pp, cc_ = idx
                c0 = cc_.start or 0
                c1 = cc_.stop if cc_.stop is not None else 512
                return psall[pp, self.b * 512 + c0: self.b * 512 + c1]

        ps = [_Bank(i) for i in range(8)]

        def psap(b, off, dims):
            return bass.AP(psall, b * 512 + off, [[4096, 128]] + dims)

        def MM(out, lhsT, rhs, start, stop, reads, writes):
            S.op("pe", lambda e: e.matmul(out, lhsT, rhs, start=start, stop=stop), reads, writes)

        def TR(out, in_, idn, reads, writes):
            S.op("pe", lambda e: e.transpose(out, in_, idn), reads, writes)

        def ACT(out, in_, func, reads, writes, bias=0.0, scale=1.0):
            S.op("act", lambda e: e.activation(out, in_, func, bias=bias, scale=scale), reads, writes)

        def ACOPY(out, in_, reads, writes):
            S.op("act", lambda e: e.copy(out, in_), reads, writes)

        def VCOPY(out, in_, reads, writes, eng="dve"):
            S.op(eng, lambda e: e.tensor_copy(out, in_), reads, writes)

        def TT(out, in0, in1, op, reads, writes, eng="dve"):
            S.op(eng, lambda e: e.tensor_tensor(out, in0, in1, op), reads, writes)

        def TS(out, in0, s1, s2, op0, op1, reads, writes, eng="dve"):
            if s2 is None:
                S.op(eng, lambda e: e.tensor_scalar(out, in0, s1, None, op0), reads, writes)
            else:
                S.op(eng, lambda e: e.tensor_scalar(out, in0, s1, s2, op0, op1), reads, writes)

        def STT(out, in0, scalar, in1, op0, op1, reads, writes, eng="dve"):
            S.op(eng, lambda e: e.scalar_tensor_tensor(out, in0, scalar, in1, op0, op1), reads, writes)

        def RECIP(out, in_, reads, writes):
            S.op("dve", lambda e: e.reciprocal(out, in_), reads, writes)

        def MEMSET(out, val, reads, writes, eng="dve"):
            S.op(eng, lambda e: e.memset(out, val), reads, writes)

        def DMA(eng, sem, out, in_, reads, writes):
            return S.dma(eng, sem, lambda e: e.dma_start(out=out, in_=in_), reads, writes)

        bank_ctr = {}

        def nb(pool=7, base=0):
            k = bank_ctr.get((pool, base), 0)
            bank_ctr[(pool, base)] = k + 1
            return base + (k % pool)

        def PK(b):
            return ("ps", b)

        def pg(a, b=None):
            if b is None:
                b = a + 1
            return [("pg", i) for i in range(a, b)]

        def page(p, n=T):
            return AR[:, p * 512: p * 512 + n]

        tmp_ctr = [0]

        def ntmp():
            tmp_ctr[0] += 1
            return tmp_ctr[0] % 3

        def TK(i):
            return ("tmp", i)

        ring_ctr = [0]

        def wload(src_ap, nelem, rkeys):
            s = ring_ctr[0] % NSLOT
            ring_ctr[0] += 1
            DMA("sp", f"ring{s}", ring[s][:, 0:nelem], src_ap, rkeys, [("ring", s)])
            return ring[s], ("ring", s)

        def dbg(name, ap_sb, shape, rkeys):
            if debug is None or name not in debug:
                return
            t = nc.dram_tensor("dbg_" + name, list(shape), F32, kind="ExternalOutput").ap()
            dbg_out[name] = DMA("sp", "dbg", t, ap_sb, rkeys, [("dbg", name)])

        def cast(sem, dst, src, wkey):
            DMA("pool", sem, dst, src, [], [wkey])

        wkeys = {}

        def convert_ffn(l, f):
            keys = []
            for j in range(NJ):
                for wi_, (wd, off) in enumerate(((w1_d, 0), (w3_d, 128))):
                    k = ("up", l, f, j, wi_)
                    cast(f"cv_up{l}{f}", up_d[l][f][j, :, :, off:off + 128],
                         wd[l, f, :, j * 128:(j + 1) * 128].rearrange("(kc p) m -> p kc m", p=128), k)
                    keys.append(k)
            wkeys[("up", l, f)] = keys
            keys = []
            for d in range(8):
                for hh in range(2):
                    k = ("dn", l, f, d, hh)
                    cast(f"cv_dn{l}{f}", dn_d[l][f][2 * d + hh],
                         w2_d[l, f, hh * 1408:(hh + 1) * 1408, d * 128:(d + 1) * 128].rearrange("(j p) m -> p j m", p=128), k)
                    keys.append(k)
            wkeys[("dn", l, f)] = keys

        def convert_mix(l):
            keys = []
            for u in range(31):
                k = ("wi", l, u)
                keys.append(k)
                if u < 2:
                    for e in range(2):
                        c = 2 * u + e
                        for hh, head in enumerate((c, 4 + c)):
                            kk = ("wi", l, u, e, hh)
                            cast(f"cv_wi{l}", wi_s[l][u, :, :, e * 128 + hh * 64: e * 128 + hh * 64 + 64],
                                 win_d[l, :, head * 64:(head + 1) * 64].rearrange("(kc p) m -> p kc m", p=128), kk)
                            keys.append(kk)
                else:
                    cast(f"cv_wi{l}", wi_s[l][u],
                         win_d[l, :, u * 256:(u + 1) * 256].rearrange("(kc p) m -> p kc m", p=128), k)
            wkeys[("wi", l)] = keys
            keys = []
            for n in range(4):
                for dcp in range(4):
                    if n == 0:
                        for kc in range(4):
                            for hh, head in enumerate((kc, 4 + kc)):
                                k = ("wb", l, n, dcp, kc, hh)
                                cast(f"cv_wb{l}", wb_s[l][n * 4 + dcp, hh * 64:(hh + 1) * 64, kc, :],
                                     wb_d[l, 0, head * 64:(head + 1) * 64, dcp * 256:(dcp + 1) * 256], k)
                                keys.append(k)
                    else:
                        k = ("wb", l, n, dcp)
                        cast(f"cv_wb{l}", wb_s[l][n * 4 + dcp],
                             wb_d[l, n, :, dcp * 256:(dcp + 1) * 256].rearrange("(kc p) m -> p kc m", p=128), k)
                        keys.append(k)
            wkeys[("wb", l)] = keys
            keys = []
            for u in range(4):
                k = ("wo", l, u)
                cast(f"cv_wo{l}", wo_s[l][u],
                     wo_d[l, :, u * 256:(u + 1) * 256].rearrange("(kc p) m -> p kc m", p=128), k)
                keys.append(k)
            wkeys[("wo", l)] = keys

        for l in range(2):
            convert_ffn(l, 0)
            convert_mix(l)
            convert_ffn(l, 1)

        DMA("sp", "c_id", ident[:], ident_d, [], ["ident"])
        DMA("sp", "c_rm", rmat[:], rmat_d, [], ["rmat"])
        DMA("sp", "c_bo", tmp[0][:, 0:128], bones_d, [], [TK(0)])
        DMA("sp", "c_cm", cmat[:], cmat_d, [], ["cmat"])
        MEMSET(onesf[:], 1.0, [], ["onesf"])
        MEMSET(hT[:], 1.0, [], [("hT", c) for c in range(8)])
        MEMSET(epst[:], EPS, [], ["epst"])
        MEMSET(onesb[:], 1.0, [], ["onesb"])
        VCOPY(bonesb[:], tmp[0][:, 0:128], [TK(0)], ["bonesb"])
        MEMSET(Va[:, :, 64:128], 1.0, [], ["Va_ones"])

        stg_ctr = [0]

        def loadT(dst, rows_src_list, R):
            si = stg_ctr[0] % 4
            stg_ctr[0] += 1
            for (r0, nr, src) in rows_src_list:
                DMA("sp", f"stg{si}", rstd[r0:r0 + nr, si * 128:(si + 1) * 128], src, [], [("stage", si)])
            b = nb()
            TR(ps[b][:, 0:R], rstd[0:R, si * 128:(si + 1) * 128], ident[0:R, 0:R], [("stage", si), "ident"], [PK(b)])
            VCOPY(dst, ps[b][:, 0:R], [], [PK(b), "params"])

        loadT(scT[:], [(0, 16, cc_d.rearrange("w (c p) -> (w c) p", p=128))], 16)
        ACT(scT[:], scT[:], AF.Silu, [], ["params", "scT"])
        loadT(fgT[:], [(0, 8, fg_d.rearrange("(c p) -> c p", p=128))], 8)
        for l in range(2):
            loadT(tmp[0][:, 256 + l * 72:256 + (l + 1) * 72], [(0, 72, ada_b[l].rearrange("(r p) -> r p", p=128))], 72)
            loadT(ngT[:, l, :], [(0, 24, norm_g[l].rearrange("j (c p) -> (j c) p", p=128))], 24)
            loadT(cwT[:, l, :], [(0, 124, cw_d[l].rearrange("k (c p) -> (k c) p", p=128))], 124)
            loadT(cpT[:, l, :], [(0, 4, cb_d[l].rearrange("(c p) -> c p", p=128)),
                                 (4, 4, clg_d[l].rearrange("(c p) -> c p", p=128)),
                                 (8, 4, clb_d[l].rearrange("(c p) -> c p", p=128))], 12)
            TS(cwT[:, l, :], cwT[:, l, :], 0.5, None, ALU.mult, None, [], ["params"])
            for hh in range(2):
                DMA("sp", "c1", qkg[hh * 64:(hh + 1) * 64, l, :], bass.AP(qkn_d, l * 128, [[1, 64], [64, 2]]), [], ["params"])
            DMA("sp", "c1", sgs[:, l:l + 1], bass.AP(sub_d, l * 128, [[1, 128], [1, 1]]), [], ["params"])
            DMA("sp", "c2", tmp[1 + l][:, 0:256], bass.AP(lam_d, l * 256, [[0, 128], [1, 256]]), [], [TK(1 + l)])
        S.barrier(skip="cv_")
        for l in range(2):
            lam_init = 0.8 - 0.6 * math.exp(-0.3 * l)
            for q in range(2):
                TT(tmp[0][:, 0:64], tmp[1 + l][:, q * 128:q * 128 + 64], tmp[1 + l][:, q * 128 + 64:q * 128 + 128], ALU.mult,
                   [TK(1 + l)], [TK(0)])
                S.op("dve", lambda e, o=lsm[:, l * 2 + q:l * 2 + q + 1], i=tmp[0][:, 0:64]:
                     e.reduce_sum(o, i, mybir.AxisListType.X), [TK(0)], ["lsm"])
            ACT(lsm[:, l * 2:l * 2 + 2], lsm[:, l * 2:l * 2 + 2], AF.Exp, [], ["lsm"])
            STT(nlam[:, l:l + 1], lsm[:, l * 2 + 1:l * 2 + 2], -lam_init, lsm[:, l * 2:l * 2 + 1], ALU.add, ALU.subtract,
                ["lsm"], ["params"])
            TS(sgs[:, l:l + 1], sgs[:, l:l + 1], 1.0 - lam_init, None, ALU.mult, None, [], ["params"])

        for l in range(2):
            bk = nb()
            for blk in range(18):
                sidx = blk % 2
                DMA("sp", f"ada{sidx}", bass.AP(Vcf, sidx * 4096, [[NKC * 256, 128], [512, 8], [1, 512]]),
                    ada_w[l, :, blk * 512:(blk + 1) * 512].rearrange("(kc p) n -> p kc n", p=128),
                    [], [("adastg", sidx)])
                for mc in range(4):
                    m = blk * 4 + mc
                    for kc in range(8):
                        MM(ps[bk][:, 2 * m:2 * m + 2],
                           bass.AP(Vcf, sidx * 4096 + kc * 512 + mc * 128, [[NKC * 256, 128], [1, 128]]),
                           bass.AP(scT, kc, [[16, 128], [8, 2]]),
                           kc == 0, kc == 7, [("adastg", sidx), "scT", "params"], [PK(bk)])
            for w in range(2):
                TT(modT[:, l, :, w], psap(bk, w, [[2, 72]]), tmp[0][:, 256 + l * 72:256 + (l + 1) * 72], ALU.add,
                   ["params"], [PK(bk), "mod"])
        for l in range(2):
            for w in range(2):
                for idx in range(3):
                    TS(Gm[:, l, w, idx, :], modT[:, l, (3 * idx + 1) * 8:(3 * idx + 2) * 8, w], 1.0, None, ALU.add, None,
                       ["mod"], ["G"])
                    TT(Gm[:, l, w, idx, :], Gm[:, l, w, idx, :], ngT[:, l, idx * 8:(idx + 1) * 8], ALU.mult,
                       ["params", "G"], ["G"])
                    TS(Gh[:, l, w, idx, :], modT[:, l, (3 * idx + 2) * 8:(3 * idx + 3) * 8, w], 0.5, None, ALU.mult, None,
                       ["mod"], ["G"])
        S.barrier(skip="cv_")

        def HK(c):
            return ("hT", c)

        HKall = [("hT", c) for c in range(8)]
        STATB = 7

        def stat_sq(c, nchunks, n, cols=HALO):
            ti = ntmp()
            ACT(tmpb[ti][:, 0:n], hT[:, c, cols:cols + n], AF.Square, [HK(c)], [TK(ti)])

            def mm():
                MM(ps[STATB][:, 0:n], onesb[:], tmpb[ti][:, 0:n], c == 0, c == nchunks - 1, [TK(ti), "onesb"], [PK(STATB)])
            return mm

        def rms_rstd(src_fn, nchunks, cols, n, inv_n, lhs, lhs_key, skeys_fn, halo=False, pre=False):
            bk = STATB if pre else nb()
            bh = nb() if halo else None
            for c in range(nchunks):
                if pre:
                    break
                ti = ntmp()
                if halo:
                    ACT(tmpb[ti][:, 0:WT], src_fn(c, 0, WT), AF.Square, skeys_fn(c), [TK(ti)])
                    MM(ps[bk][:, 0:T], lhs, tmpb[ti][:, HALO:HALO + T], c == 0, c == nchunks - 1, [TK(ti), lhs_key], [PK(bk)])
                    MM(ps[bh][:, 0:32], lhs, bass.AP(tmpb[ti], 0, [[2 * WT, 128], [HALO + T, 2], [1, 16]]),
                       c == 0, c == nchunks - 1, [TK(ti), lhs_key], [PK(bh)])
                else:
                    ACT(tmpb[ti][:, 0:n], src_fn(c, cols, n), AF.Square, skeys_fn(c), [TK(ti)])
                    MM(ps[bk][:, 0:n], lhs, tmpb[ti][:, 0:n], c == 0, c == nchunks - 1, [TK(ti), lhs_key], [PK(bk)])
            if halo:
                ACT(rstd[:, HALO:HALO + T], ps[bk][:, 0:T], AF.Ln, [], [PK(bk), "rstd"], bias=epst[:, 0:1], scale=inv_n)
                ACT(bass.AP(rstd, 0, [[WT, 128], [HALO + T, 2], [1, 16]]), psap(bh, 0, [[16, 2], [1, 16]]), AF.Ln, [], [PK(bh), "rstd"],
                    bias=epst[:, 0:1], scale=inv_n)
                ACT(rstd[:, 0:WT], rstd[:, 0:WT], AF.Exp, [], ["rstd"], scale=-0.5)
            else:
                ACT(rstd[:, cols:cols + n], ps[bk][:, 0:n], AF.Ln, [], [PK(bk), "rstd"], bias=epst[:, 0:1], scale=inv_n)
                ACT(rstd[:, cols:cols + n], rstd[:, cols:cols + n], AF.Exp, [], ["rstd"], scale=-0.5)

        def norm_mod(l, w, idx, cols, n, halo=False, pre=False):
            rms_rstd(lambda c, c0, nn: hT[:, c, c0:c0 + nn], 8, cols, n, 1.0 / D, onesb[:], "onesb", lambda c: [HK(c)], halo, pre)
            c0, nn = (0, WT) if halo else (cols, n)
            for c in range(8):
                ti = ntmp()
                STT(tmp[ti][:, 0:nn], hT[:, c, c0:c0 + nn], Gm[:, l, w, idx, c:c + 1], rstd[:, c0:c0 + nn], ALU.mult, ALU.mult,
                    [HK(c), "rstd", "G"], [TK(ti)])
                ACT(nT[:, c, c0:c0 + nn], tmp[ti][:, 0:nn], AF.Identity, [TK(ti), "mod"], [("nT", c)],
                    bias=modT[:, l, 3 * idx * 8 + c, w:w + 1])

        def ffn(l, f, w, idx, n, pre=False, post=None):
            cols = HALO
            norm_mod(l, w, idx, cols, n, pre=pre)
            nTk = [("nT", c) for c in range(8)]
            for j in range(NJ):
                slot, sk = wload(up_d[l][f][j].rearrange("p kc m -> p (kc m)"), 2048, wkeys[("up", l, f)])
                ba, bb = nb(), nb()
                for kc in range(8):
                    MM(ps[ba][:, 0:n], slot[:, kc * 256:kc * 256 + 128], nT[:, kc, cols:cols + n], kc == 0, kc == 7,
                       [sk, ("nT", kc)], [PK(ba), PK(bb)] if kc == 0 else [PK(ba)])
                for kc in range(8):
                    MM(ps[bb][:, 0:n], slot[:, kc * 256 + 128:kc * 256 + 256], nT[:, kc, cols:cols + n], kc == 0, kc == 7,
                       [sk, ("nT", kc)], [PK(bb)])
                ti = ntmp()
                ACT(tmp[ti][:, 0:n], ps[ba][:, 0:n], AF.Silu, [], [PK(ba), TK(ti)])
                TT(page(j, n), tmp[ti][:, 0:n], ps[bb][:, 0:n], ALU.mult, [TK(ti)], [PK(bb)] + pg(j))
            pend = [None]
            for d in range(8):
                bo = nb()
                sl2 = [wload(dn_d[l][f][2 * d + hh].rearrange("p j m -> p (j m)"), 1408, wkeys[("dn", l, f)]) for hh in range(2)]
                for hh in range(2):
                    slot, sk = sl2[hh]
                    for jj in range(11):
                        j = hh * 11 + jj
                        MM(ps[bo][:, 0:n], slot[:, jj * 128:(jj + 1) * 128], page(j, n), j == 0, j == NJ - 1,
                           ([sl2[0][1], sl2[1][1]] if j == 0 else [sk]) + pg(j), [PK(bo)])
                    if hh == 0 and pend[0] is not None:
                        pend[0]()
                        pend[0] = None
                STT(hT[:, d, cols:cols + n], ps[bo][:, 0:n], Gh[:, l, w, idx, d:d + 1], hT[:, d, cols:cols + n],
                    ALU.mult, ALU.add, ["G"], [PK(bo), HK(d)])
                if post is not None:
                    pend[0] = post(d)
            if pend[0] is not None:
                pend[0]()

        def load_rope(tile):
            DMA("sp", "ropec", cosb[:], rope_d[0, :, tile * T:(tile + 1) * T], [], ["cosb"])
            DMA("sp", "ropes", sinb[:], rope_d[1, :, tile * T:(tile + 1) * T], [], ["sinb"])

        def qk_post(bk, n, dst, dkeys, gain, rope):
            xi, si, ci = 0, 1, 2
            xf = tmp[xi]
            if gain is not None:
                ACT(tmpb[si][:, 0:n], ps[bk][:, 0:n], AF.Square, [], [PK(bk), TK(si)])
                ACOPY(xf[:, 0:n], ps[bk][:, 0:n], [], [PK(bk), TK(xi)])
                b2 = nb()
                MM(ps[b2][:, 0:n], bonesb[:], tmpb[si][:, 0:n], True, True, [TK(si), "bonesb"], [PK(b2)])
                ACT(tmp[si][:, 0:n], ps[b2][:, 0:n], AF.Ln, [], [PK(b2), TK(si)], bias=epst[:, 0:1], scale=1.0 / 64)
                ACT(tmp[si][:, 0:n], tmp[si][:, 0:n], AF.Exp, [], [TK(si)], scale=-0.5)
                STT(xf[:, 0:n], xf[:, 0:n], gain, tmp[si][:, 0:n], ALU.mult, ALU.mult, [TK(si), "params"], [TK(xi)])
            else:
                ACOPY(xf[:, 0:n], ps[bk][:, 0:n], [], [PK(bk), TK(xi)])
            if rope:
                b3 = nb()
                MM(ps[b3][:, 0:n], rmat[:], xf[:, 0:n], True, True, [TK(xi), "rmat"], [PK(b3)])
                TT(tmp[ci][:, 0:n], xf[:, 0:n], cosb[:, 0:n], ALU.mult, [TK(xi), "cosb"], [TK(ci)])
                TT(tmp[si][:, 0:n], ps[b3][:, 0:n], sinb[:, 0:n], ALU.mult, ["sinb"], [PK(b3), TK(si)])
                TT(dst, tmp[ci][:, 0:n], tmp[si][:, 0:n], ALU.add, [TK(ci), TK(si)], dkeys)
            else:
                VCOPY(dst, xf[:, 0:n], [TK(xi)], dkeys)

        def proj_fm(l, u, e, n, cols=HALO):
            raise NotImplementedError

        nTk_all = [("nT", c) for c in range(8)]

        def load_tile_x(src, tok0, n):
            ns = n // 128
            DMA("sp", "xin", bass.AP(ARf, 0, [[24 * 256, 128], [1024, ns], [1, 1024]]),
                src[tok0:tok0 + n, :].rearrange("(s p) f -> p s f", p=128), [], pg(0, 16))
            pendx = [None]
            for c in range(8):
                b = nb()
                for s_ in range(ns):
                    TR(ps[b][:, s_ * 128:(s_ + 1) * 128], ARf[:, s_ * 1024 + c * 128: s_ * 1024 + (c + 1) * 128], ident[:],
                       pg(0, 16) + ["ident"], [PK(b)])
                if pendx[0] is not None:
                    pendx[0]()
                VCOPY(hT[:, c, HALO:HALO + n], ps[b][:, 0:n], [], [PK(b), HK(c)])
                pendx[0] = stat_sq(c, 8, n)
            pendx[0]()

        def hkey(which, l, t, c):
            return (which, t, c)

        def sweep1(l, tile):
            w = 1 if tile == 8 else 0
            n = CTX if w else T
            tok0 = SEQ if w else tile * T
            kc0 = tok0 // 128
            cols = HALO
            if l == 0:
                load_tile_x(ctx_d if w else x_d, 0 if w else tok0, n)
            else:
                for c in range(8):
                    DMA("sp", f"hin{c}", hT[:, c, cols:cols + n], h2_d[:, c, tok0:tok0 + n], [hkey("h2", 0, tile, c)], [HK(c)])
            if not w:
                load_rope(tile)

            def post1(d):
                if not (l == 1 and w == 1):
                    DMA("act", f"hout{d}", h1_d[:, d, tok0:tok0 + n], hT[:, d, cols:cols + n], [HK(d)], [hkey("h1", l, tile, d)])
                return stat_sq(d, 8, n)

            ffn(l, 0, w, 0, n, pre=(l == 0), post=post1)
            norm_mod(l, w, 1, cols, n, pre=True)
            wk = wkeys[("wi", l)]
            slot, sk = wload(wi_s[l][2].rearrange("p kc m -> p (kc m)"), 2048, wk)
            b = nb()
            for kc in range(8):
                MM(ps[b][:, 0:n], slot[:, kc * 256:kc * 256 + 128], nT[:, kc, cols:cols + n], kc == 0, kc == 7,
                   [sk, ("nT", kc)], [PK(b)])
            qk_post(b, n, KaT[:, tok0:tok0 + n], [("KaT", tile)], qkg[:, l, 1:2], rope=(w == 0))
            b = nb()
            ns = n // 128
            for s_ in range(ns):
                for kc in range(8):
                    MM(ps[b][:, s_ * 128:(s_ + 1) * 128], nT[:, kc, cols + s_ * 128:cols + (s_ + 1) * 128],
                       slot[:, kc * 256 + 128:kc * 256 + 256], kc == 0, kc == 7, [sk, ("nT", kc)], [PK(b)])
            VCOPY(Va[:, kc0:kc0 + ns, 0:64], psap(b, 0, [[128, ns], [1, 64]]), ["Va_ones"],
                  [PK(b), ("Va", tile)])
            ACOPY(Va[:, kc0:kc0 + ns, 128:192], psap(b, 64, [[128, ns], [1, 64]]), ["Va_ones"],
                  [PK(b), ("Va", tile)])
            for uu in range(2):
                slot, sk = wload(wi_s[l][9 + uu].rearrange("p kc m -> p (kc m)"), 2048, wk)
                for e in range(2):
                    h = 2 * uu + e
                    b = nb()
                    for kc in range(8):
                        MM(ps[b][:, 0:n], slot[:, kc * 256 + e * 128:kc * 256 + (e + 1) * 128], nT[:, kc, cols:cols + n],
                           kc == 0, kc == 7, [sk, ("nT", kc)], [PK(b)])
                    qk_post(b, n, KcT[:, h, tok0:tok0 + n], [("KcT", tile, h)], None, rope=(w == 0))
            for (ubase, dst, dk) in ((11, Vc, "Vc"), (13, Zc, "Zc")):
                for uu in range(2):
                    slot, sk = wload(wi_s[l][ubase + uu].rearrange("p kc m -> p (kc m)"), 2048, wk)
                    for s2 in range(0, ns, 2):
                        b = nb()
                        for s_ in range(s2, s2 + 2):
                            for kc in range(8):
                                MM(ps[b][:, (s_ - s2) * 256:(s_ - s2 + 1) * 256],
                                   nT[:, kc, cols + s_ * 128:cols + (s_ + 1) * 128],
                                   slot[:, kc * 256:(kc + 1) * 256], kc == 0, kc == 7, [sk, ("nT", kc)], [PK(b)])
                        o = dst[:, kc0 + s2:kc0 + s2 + 2, uu * 256:(uu + 1) * 256]
                        i = psap(b, 0, [[256, 2], [1, 256]])
                        if (s2 // 2 + uu) % 2 == 0:
                            VCOPY(o, i, [], [PK(b), (dk, tile, uu)])
                        else:
                            ACOPY(o, i, [], [PK(b), (dk, tile, uu)])

        def all_keys(name, tiles, extra=None):
            ks = []
            for t in tiles:
                if extra is None:
                    ks.append((name, t))
                else:
                    for x in extra:
                        ks.append((name, t, x))
            return ks

        def sweep2(l, tile):
            w = 1 if tile == 8 else 0
            n = CTX if w else T
            tok0 = SEQ if w else tile * T
            cols = HALO
            ktiles = [8] if w else list(range(9))
            kcs = [32, 33] if w else list(range(NKC))
            first = (tile == 0) or w
            last = (tile == 7) or w
            halo = not w
            lo = tok0 - (0 if first else HALO)
            hi = tok0 + n + (0 if last else HALO)
            c_lo = cols - (tok0 - lo)
            wk = wkeys[("wi", l)]
            pre_units = {0: (wload(wi_s[l][3].rearrange("p kc m -> p (kc m)"), 2048, wk),
                             wload(wi_s[l][5].rearrange("p kc m -> p (kc m)"), 2048, wk))}
            for c in range(8):
                rk = [hkey("h1", l, t, c) for t in ([8] if w else range(max(0, tile - 1), min(7, tile + 1) + 1))]
                DMA("sp", f"hin{c}", hT[:, c, c_lo:c_lo + (hi - lo)], h1_d[:, c, lo:hi], rk, [HK(c)])
            if not w:
                load_rope(tile)
            norm_mod(l, w, 1, cols, n, halo=halo)
            wk = wkeys[("wi", l)]
            GL0 = 0
            AC0 = 4096
            glk = pg(0, 9)
            ack = pg(16, 24)

            def glu(cc, a, b_=None):
                return ARf[:, GL0 + cc * WT + a: GL0 + cc * WT + (b_ if b_ is not None else a)]

            for uu in range(2):
                if uu in pre_units:
                    (sa, ska), (sg_, skg) = pre_units[uu]
                else:
                    sa, ska = wload(wi_s[l][3 + uu].rearrange("p kc m -> p (kc m)"), 2048, wk)
                    sg_, skg = wload(wi_s[l][5 + uu].rearrange("p kc m -> p (kc m)"), 2048, wk)
                for e in range(2):
                    cc = 2 * uu + e
                    ba, bg = nb(), nb()
                    for (bx, sl, skx) in ((ba, sa, ska), (bg, sg_, skg)):
                        for kc in range(8):
                            MM(ps[bx][:, 0:n], sl[:, kc * 256 + e * 128:kc * 256 + (e + 1) * 128], nT[:, kc, cols:cols + n],
                               kc == 0, kc == 7, [skx, ("nT", kc)], [PK(bx)])
                    ti = ntmp()
                    ACT(tmp[ti][:, 0:n], ps[bg][:, 0:n], AF.Tanh, [], [PK(bg), TK(ti)], scale=0.5)
                    STT(glu(cc, cols, cols + n), tmp[ti][:, 0:n], 1.0, ps[ba][:, 0:n], ALU.add, ALU.mult, [TK(ti)],
                        [PK(ba)] + glk)
                    if halo:
                        bh = nb()
                        for (c0h, sl, skx) in ((0, sa, ska), (32, sg_, skg)):
                            for kc in range(8):
                                MM(ps[bh][:, c0h:c0h + 30], sl[:, kc * 256 + e * 128:kc * 256 + (e + 1) * 128],
                                   bass.AP(nT, kc * WT + 1, [[8 * WT, 128], [HALO + T - 1, 2], [1, 15]]),
                                   kc == 0, kc == 7, [skx, ("nT", kc)], [PK(bh)])
                        ti = ntmp()
                        ACT(tmp[ti][:, 0:30], ps[bh][:, 32:62], AF.Tanh, [], [PK(bh), TK(ti)], scale=0.5)
                        STT(bass.AP(ARf, GL0 + cc * WT + 1, [[24 * 256, 128], [HALO + T - 1, 2], [1, 15]]),
                            bass.AP(tmp[ti], 0, [[WT, 128], [15, 2], [1, 15]]), 1.0,
                            psap(bh, 0, [[15, 2], [1, 15]]), ALU.add, ALU.mult, [TK(ti)],
                            [PK(bh)] + glk)
                    if first:
                        MEMSET(glu(cc, 0, HALO), 0.0, [], glk)
                    if last:
                        MEMSET(glu(cc, cols + n, cols + n + HALO), 0.0, [], glk)
            for k in range(31):
                for cc in range(4):
                    acc = ARf[:, AC0 + cc * 512: AC0 + cc * 512 + n]
                    gk = pg((cc * WT) // 256, ((cc + 1) * WT + 255) // 256)
                    ak = pg(16 + 2 * cc, 18 + 2 * cc)
                    if k == 0:
                        TS(acc, glu(cc, 1, 1 + n), cwT[:, l, cc:cc + 1], cpT[:, l, cc:cc + 1], ALU.mult, ALU.add,
                           gk + ["params"], ak)
                    else:
                        STT(acc, glu(cc, 1 + k, 1 + k + n), cwT[:, l, k * 4 + cc:k * 4 + cc + 1], acc, ALU.mult, ALU.add,
                            gk + ["params"], ak)
            zk = all_keys("Zc", ktiles[:-1] if not w else ktiles, (0, 1))
            nunits = 1 if w else 16
            for u in range(nunits):
                if w:
                    slot, sk = wload(dftC_d, 1024, [])
                else:
                    slot, sk = wload(dftL_d[tile, u], 2048, [])
                for tcl in range(2):
                    tc = (32 + tcl) if w else (2 * u + tcl)
                    off = tcl * (512 if w else 1024)
                    for g in range(4):
                        for q in range(2):
                            MM(ps[2 * g + q][:, 0:n], Zc[:, tc, g * 128:(g + 1) * 128],
                               slot[:, off + q * n: off + q * n + n],
                               (u == 0 and tcl == 0), (u == nunits - 1 and tcl == 1), [sk] + zk, [PK(2 * g + q)])
            for gp in range(2):
                for gg in range(2):
                    g = 2 * gp + gg
                    for q in range(2):
                        ACOPY(tmpb[gg][:, q * WT:q * WT + n], ps[2 * g + q][:, 0:n], [], [PK(2 * g + q), TK(gg)])
                for gg in range(2):
                    g = 2 * gp + gg
                    b = 2 * g
                    MM(ps[b][:, 0:n], cmat[:, 0:128], tmpb[gg][:, 0:n], True, False, ["cmat", TK(gg)], [PK(b)])
                    MM(ps[b][:, 0:n], cmat[:, 128:256], tmpb[gg][:, WT:WT + n], False, True, ["cmat", TK(gg)], [PK(b)])
                    ACOPY(page(9 + g, n), ps[b][:, 0:n], [], [PK(b)] + pg(9 + g))
            bm = nb()
            for cc in range(4):
                MM(ps[bm][:, 0:n], onesf[:], ARf[:, AC0 + cc * 512: AC0 + cc * 512 + n], cc == 0, cc == 3,
                   ack + ["onesf"], [PK(bm)])
            for cc in range(4):
                acc = ARf[:, AC0 + cc * 512: AC0 + cc * 512 + n]
                STT(acc, ps[bm][:, 0:n], -1.0 / 512, acc, ALU.mult, ALU.add, [], [PK(bm)] + ack)
            rms_rstd(lambda c, c0, nn: ARf[:, AC0 + c * 512: AC0 + c * 512 + nn], 4, cols, n, 1.0 / 512, onesb[:], "onesb", lambda c: ack)
            for cc in range(4):
                acc = ARf[:, AC0 + cc * 512: AC0 + cc * 512 + n]
                TT(acc, acc, rstd[:, cols:cols + n], ALU.mult, ["rstd"], ack)
                ACT(page(cc, n), acc, AF.Silu, ack + ["params"], pg(cc), bias=cpT[:, l, 8 + cc:9 + cc],
                    scale=cpT[:, l, 4 + cc:5 + cc])
            kak = all_keys("KaT", ktiles)
            vak = all_keys("Va", ktiles) + ["Va_ones"]
            for uu in range(2):
                slot, sk = wload(wi_s[l][uu].rearrange("p kc m -> p (kc m)"), 2048, wk)
                for e in range(2):
                    c = 2 * uu + e
                    b = nb()
                    for kc in range(8):
                        MM(ps[b][:, 0:n], slot[:, kc * 256 + e * 128:kc * 256 + (e + 1) * 128], nT[:, kc, cols:cols + n],
                           kc == 0, kc == 7, [sk, ("nT", kc)], [PK(b)])
                    qk_post(b, n, page(16 + c, n), pg(16 + c), qkg[:, l, 0:1], rope=(w == 0))
            SC = 0.125

            def groups_of(G):
                return [kcs[i:i + G] for i in range(0, len(kcs), G)]

            def exp_group(sb0, ep0, g):
                ACT(bass.AP(AR, ep0 * 512, [[24 * 512, 128], [512, g], [1, n]]), psap(sb0, 0, [[512, g], [1, n]]), AF.Exp, [],
                    [PK(sb0 + j) for j in range(g)] + pg(ep0, ep0 + g), scale=SC)

            grpsA = groups_of(3)
            for c in range(4):
                for hh in range(2):
                    p0 = hh * 64
                    ba = 6 + nb(2, 100) - 100

                    def issue_sA(gi):
                        sb0 = (gi % 2) * 3
                        for j, kc in enumerate(grpsA[gi]):
                            MM(ps[sb0 + j][:, 0:n], KaT[p0:p0 + 64, kc * 128:(kc + 1) * 128], page(16 + c, n)[p0:p0 + 64, :],
                               True, True, kak + pg(16 + c), [PK(sb0 + j)])

                    issue_sA(0)
                    for gi, grp in enumerate(grpsA):
                        if gi + 1 < len(grpsA):
                            issue_sA(gi + 1)
                        sb0 = (gi % 2) * 3
                        ep0 = (13, 20)[gi % 2]
                        exp_group(sb0, ep0, len(grp))
                        for j, kc in enumerate(grp):
                            MM(ps[ba][:, 0:n], Va[:, kc, hh * 64:hh * 64 + 128], page(ep0 + j, n),
                               gi == 0 and j == 0, gi == len(grpsA) - 1 and j == len(grp) - 1,
                               vak + pg(ep0 + j), [PK(ba)])
                    ti = ntmp()
                    q0 = 64 - p0
                    RECIP(tmp[ti][q0:q0 + 64, 0:n], ps[ba][q0:q0 + 64, 0:n], [], [PK(ba), TK(ti)])
                    TT(page(4 + c, n)[p0:p0 + 64, :], ps[ba][p0:p0 + 64, 0:n], tmp[ti][q0:q0 + 64, 0:n], ALU.mult,
                       [TK(ti)], [PK(ba)] + pg(4 + c))
            kck = all_keys("KcT", ktiles, range(4))
            vck = all_keys("Vc", ktiles, (0, 1))
            for uu in range(2):
                slot, sk = wload(wi_s[l][7 + uu].rearrange("p kc m -> p (kc m)"), 2048, wk)
                for e in range(2):
                    h = 2 * uu + e
                    b = nb()
                    for kc in range(8):
                        MM(ps[b][:, 0:n], slot[:, kc * 256 + e * 128:kc * 256 + (e + 1) * 128], nT[:, kc, cols:cols + n],
                           kc == 0, kc == 7, [sk, ("nT", kc)], [PK(b)])
                    qk_post(b, n, page(16 + h, n), pg(16 + h), None, rope=(w == 0))
            grpsC = groups_of(2)
            pending = [None]

            def make_final(h):
                def fin():
                    RECIP(tmp[1][:, 0:n], tmp[1][:, 0:n], [], [TK(1)])
                    TT(tmp[0][:, 0:n], tmp[0][:, 0:n], tmp[1][:, 0:n], ALU.mult, [TK(1)], [TK(0)])
                    RECIP(rstd[:, 0:n], rstd[:, 0:n], [], ["rstd"])
                    TT(tmp[2][:, 0:n], tmp[2][:, 0:n], rstd[:, 0:n], ALU.mult, ["rstd"], [TK(2)])
                    STT(tmp[0][:, 0:n], tmp[2][:, 0:n], nlam[:, l:l + 1], tmp[0][:, 0:n], ALU.mult, ALU.add, [TK(2), "params"], [TK(0)])
                    ACT(tmpb[1][:, 0:n], tmp[0][:, 0:n], AF.Square, [TK(0)], [TK(1)])
                    b2 = nb(4)
                    MM(ps[b2][:, 0:n], onesb[:], tmpb[1][:, 0:n], True, True, [TK(1), "onesb"], [PK(b2)])
                    ACT(tmp[1][:, 0:n], ps[b2][:, 0:n], AF.Ln, [], [PK(b2), TK(1)], bias=epst[:, 0:1], scale=1.0 / 128)
                    ACT(tmp[1][:, 0:n], tmp[1][:, 0:n], AF.Exp, [], [TK(1)], scale=-0.5)
                    ych = (13, 14, 15, 8)[h]
                    STT(page(ych, n), tmp[0][:, 0:n], sgs[:, l:l + 1], tmp[1][:, 0:n], ALU.mult, ALU.mult,
                        [TK(0), TK(1), "params"], pg(ych))
                return fin

            for h in range(4):
                for p_ in range(2):
                    p0 = p_ * 64
                    bev, bs_ = 4 + 2 * p_, 5 + 2 * p_

                    def issue_sC(gi):
                        sb0 = (gi % 2) * 2
                        for j, kc in enumerate(grpsC[gi]):
                            MM(ps[sb0 + j][:, 0:n], KcT[p0:p0 + 64, h, kc * 128:(kc + 1) * 128], page(16 + h, n)[p0:p0 + 64, :],
                               True, True, kck + pg(16 + h), [PK(sb0 + j)])

                    issue_sC(0)
                    for gi, grp in enumerate(grpsC):
                        if gi + 1 < len(grpsC):
                            issue_sC(gi + 1)
                        sb0 = (gi % 2) * 2
                        ep0 = (20, 22)[gi % 2]
                        exp_group(sb0, ep0, len(grp))
                        for j, kc in enumerate(grp):
                            first = (gi == 0 and j == 0)
                            lastf = (gi == len(grpsC) - 1 and j == len(grp) - 1)
                            MM(ps[bev][:, 0:n], Vc[:, kc, h * 128:(h + 1) * 128], page(ep0 + j, n), first, lastf,
                               vck + pg(ep0 + j), [PK(bev)])
                        for j, kc in enumerate(grp):
                            first = (gi == 0 and j == 0)
                            lastf = (gi == len(grpsC) - 1 and j == len(grp) - 1)
                            MM(ps[bs_][:, 0:n], onesb[:], page(ep0 + j, n), first, lastf, ["onesb"] + pg(ep0 + j), [PK(bs_)])
                    if p_ == 0 and pending[0] is not None:
                        pending[0]()
                        pending[0] = None
                ACOPY(tmp[0][:, 0:n], ps[4][:, 0:n], [], [PK(4), TK(0)])
                VCOPY(tmp[1][:, 0:n], ps[5][:, 0:n], [], [PK(5), TK(1)])
                ACOPY(tmp[2][:, 0:n], ps[6][:, 0:n], [], [PK(6), TK(2)])
                VCOPY(rstd[:, 0:n], ps[7][:, 0:n], [], [PK(7), "rstd"])
                pending[0] = make_final(h)
            pending[0]()
            ypages = ((4, 5, 6, 7), (0, 1, 2, 3), (13, 14, 15, 8), (9, 10, 11, 12))
            for dcp in range(4):
                for nbr in range(4):
                    sgl, skgl = wload(wi_s[l][15 + nbr * 4 + dcp].rearrange("p kc m -> p (kc m)"), 2048, wk)
                    swb, skwb = wload(wb_s[l][nbr * 4 + dcp].rearrange("p kc m -> p (kc m)"), 1024, wkeys[("wb", l)])
                    bgs = [nb(), nb()]
                    bps = [nb(), nb()]
                    for e in range(2):
                        for kc in range(8):
                            claim = (e == 0 and kc == 0)
                            MM(ps[bgs[e]][:, 0:n], sgl[:, kc * 256 + e * 128:kc * 256 + (e + 1) * 128], nT[:, kc, cols:cols + n],
                               kc == 0, kc == 7, [skgl] + ([skwb] if claim else []) + [("nT", kc)],
                               [PK(bgs[0]), PK(bgs[1]), PK(bps[0]), PK(bps[1])] if claim else [PK(bgs[e])])
                    for e in range(2):
                        for kc in range(4):
                            MM(ps[bps[e]][:, 0:n], swb[:, kc * 256 + e * 128:kc * 256 + (e + 1) * 128], page(ypages[nbr][kc], n),
                               kc == 0, kc == 3, [skwb] + pg(ypages[nbr][kc]), [PK(bps[e])])
                    for e in range(2):
                        dc = 2 * dcp + e
                        bp, bg = bps[e], bgs[e]
                        ACT(tmp[2][:, 0:n], ps[bg][:, 0:n], AF.Tanh, [], [PK(bg), TK(2)], scale=0.5)
                        if nbr == 0:
                            STT(tmp[e][:, 0:n], tmp[2][:, 0:n], 1.0, ps[bp][:, 0:n], ALU.add, ALU.mult, [TK(2)], [PK(bp), TK(e)])
                        else:
                            STT(tmp[2][:, 0:n], tmp[2][:, 0:n], 1.0, ps[bp][:, 0:n], ALU.add, ALU.mult, [], [PK(bp), TK(2)])
                            if nbr < 3:
                                TT(tmp[e][:, 0:n], tmp[e][:, 0:n], tmp[2][:, 0:n], ALU.add, [TK(2)], [TK(e)])
                            else:
                                TT(page(16 + dc, n), tmp[e][:, 0:n], tmp[2][:, 0:n], ALU.add, [TK(2), TK(e)], pg(16 + dc))
            pendw = [None]
            for d in range(8):
                if d % 2 == 0:
                    slot, sk = wload(wo_s[l][d // 2].rearrange("p kc m -> p (kc m)"), 2048, wkeys[("wo", l)])
                bo = nb()
                for kc in range(8):
                    MM(ps[bo][:, 0:n], slot[:, kc * 256 + (d % 2) * 128:kc * 256 + (d % 2 + 1) * 128], page(16 + kc, n),
                       kc == 0, kc == 7, [sk] + pg(16 + kc), [PK(bo)])
                    if kc == 5 and pendw[0] is not None:
                        pendw[0]()
                        pendw[0] = None
                STT(hT[:, d, cols:cols + n], ps[bo][:, 0:n], Gh[:, l, w, 1, d:d + 1], hT[:, d, cols:cols + n],
                    ALU.mult, ALU.add, ["G"], [PK(bo), HK(d)])
                pendw[0] = stat_sq(d, 8, n)
            pendw[0]()

            def post2(d):
                if l == 0:
                    DMA("act", f"hout{d}", h2_d[:, d, tok0:tok0 + n], hT[:, d, cols:cols + n], [HK(d)], [hkey("h2", 0, tile, d)])
                    return None
                return stat_sq(d, 8, n)

            ffn(l, 1, w, 2, n, pre=True, post=post2)
            if l == 1:
                rms_rstd(lambda c, c0, nn: hT[:, c, c0:c0 + nn], 8, cols, n, 1.0 / D, onesb[:], "onesb", lambda c: [HK(c)], pre=True)
                for c in range(8):
                    STT(hT[:, c, cols:cols + n], hT[:, c, cols:cols + n], fgT[:, c:c + 1], rstd[:, cols:cols + n],
                        ALU.mult, ALU.mult, ["rstd", "params"], [HK(c)])
                for s_ in range(4):
                    for cg in range(2):
                        b = nb()
                        for q in range(4):
                            c = cg * 4 + q
                            TR(ps[b][:, q * 128:(q + 1) * 128], hT[:, c, cols + s_ * 128:cols + (s_ + 1) * 128], ident[:],
                               [HK(c), "ident"], [PK(b)])
                        o = ARf[:, s_ * 1024 + cg * 512: s_ * 1024 + (cg + 1) * 512]
                        if cg == 0:
                            VCOPY(o, ps[b][:, 0:512], [], [PK(b)] + pg(0, 16))
                        else:
                            ACOPY(o, ps[b][:, 0:512], [], [PK(b)] + pg(0, 16))
                DMA("sp", "oout", out_d[tok0:tok0 + n, :].rearrange("(s p) f -> p s f", p=128),
                    bass.AP(ARf, 0, [[24 * 256, 128], [1024, 4], [1, 1024]]),
                    pg(0, 16), [("out", tile)])

        for l in range(2):
            for tile in (8, 0, 1, 2, 3, 4, 5, 6, 7):
                sweep1(l, tile)
            tiles2 = list(range(8)) + ([8] if l == 0 else [])
            for tile in tiles2:
                sweep2(l, tile)
        S.barrier()
        print('instr counts', {e: len(v) for e, v in S.q.items()}, flush=True)
        S.emit()
    return nc, list(dbg_out.keys())


_CONST = {}


def _consts():
    if _CONST:
        return _CONST
    ident = np.eye(128, dtype=np.float32)
    rmat = np.zeros((128, 128), np.float32)
    for m in range(128):
        d = m % 64
        q = d // 16
        if q % 2 == 0:
            rmat[m + 16, m] = -1.0
        else:
            rmat[m - 16, m] = 1.0
    bones = np.zeros((128, 128), np.float32)
    bones[:64, :64] = 1.0
    bones[64:, 64:] = 1.0
    t = np.arange(SEQ)
    row = (t // 64).astype(np.float64)
    col = (t % 64).astype(np.float64)
    inv = 10000.0 ** (-np.arange(0, 32, 2, dtype=np.float64) / 32)
    ang = np.concatenate([row[:, None] * inv, row[:, None] * inv, col[:, None] * inv, col[:, None] * inv], axis=1)
    angT = np.concatenate([ang.T, ang.T], axis=0)
    rope = np.stack([np.cos(angT), np.sin(angT)]).astype(np.float32)
    tt = np.arange(SEQ, dtype=np.int64)
    prod = (tt[:, None] * tt[None, :]) % SEQ
    a = 2 * np.pi * prod / SEQ
    Ct = (np.cos(a) / 64.0)
    St = (np.sin(a) / 64.0)
    dftL = np.zeros((8, 16, 128, 2, 1024), dtype=ml_dtypes.bfloat16)
    for tile in range(8):
        cs = Ct[:, tile * T:(tile + 1) * T].reshape(16, 2, 128, T)
        ss = St[:, tile * T:(tile + 1) * T].reshape(16, 2, 128, T)
        blk = np.concatenate([cs, ss], axis=-1)
        dftL[tile] = blk.transpose(0, 2, 1, 3).astype(ml_dtypes.bfloat16)
    dftL = dftL.reshape(8, 16, 128, 2048)
    tc_ = np.arange(CTX, dtype=np.int64)
    ac = 2 * np.pi * ((tc_[:, None] * tc_[None, :]) % CTX) / CTX
    Cc_t = np.cos(ac) / 16.0
    Sc_t = np.sin(ac) / 16.0
    blk = np.concatenate([Cc_t.reshape(2, 128, CTX), Sc_t.reshape(2, 128, CTX)], axis=-1)
    dftC = blk.transpose(1, 0, 2).reshape(128, 1024).astype(ml_dtypes.bfloat16)
    ch = np.arange(128, dtype=np.int64)
    ach = 2 * np.pi * ((ch[:, None] * ch[None, :]) % 128) / 128
    cmat = np.concatenate([np.cos(ach) / math.sqrt(128.0), -np.sin(ach) / math.sqrt(128.0)], axis=1).astype(ml_dtypes.bfloat16)
    _CONST.update(c_ident=ident, c_rmat=rmat, c_bones=bones, c_rope=rope, c_dftL=dftL, c_dftC=dftC, c_cmat=cmat)
    return _CONST


def make_in_maps(inp):
    cst = _consts()
    shared = {k: np.ascontiguousarray(np.asarray(inp[k], dtype=np.float32)) for k in
              ("ada_w", "ada_b", "norm_g", "ffn_w1", "ffn_w3", "ffn_w2", "w_in", "qk_norm_a", "conv_w", "conv_b",
               "conv_ln_g", "conv_ln_b", "diff_lam", "diff_subln_g", "w_branch", "w_out", "final_g")}
    x = np.asarray(inp["x"], dtype=np.float32)
    c = np.asarray(inp["c"], dtype=np.float32)
    ctx = np.asarray(inp["ctx"], dtype=np.float32)
    c_ctx = np.asarray(inp["c_ctx"], dtype=np.float32)
    maps = []
    for b in range(8):
        m = dict(shared)
        m.update(cst)
        m["x"] = np.ascontiguousarray(x[b])
        m["ctx"] = np.ascontiguousarray(ctx[b])
        m["cc"] = np.ascontiguousarray(np.stack([c[b], c_ctx]))
        maps.append(m)
    return maps


_NC = {}


def kernel(**inputs):
    if "nc" not in _NC:
        _NC["nc"] = build()[0]
    nc = _NC["nc"]
    maps = make_in_maps(inputs)
    res = run_bass_kernel_spmd(nc, maps, core_ids=list(range(8)))
    out = np.stack([np.asarray(r["out"], dtype=np.float32) for r in res.results], axis=0)
    return out
```
